# Optimizing a Trainium2 kernel written in Bass

```python
import jax
import jax.numpy as jnp
from jax import lax
import numpy as np

D_MODEL = 2048
BATCH = 4
SEQ = 8192
DEPTH = 1

CTX_LEN = 256
GRID_W = 64
HEAD_DIM = D_MODEL // 16
N_HEADS_A = 8
N_KV_A = 2
GQA_GROUP = N_HEADS_A // N_KV_A
WINDOW = 128
BLOCK = 128
N_HEADS_B = 8
NB_ROWS = 8
NB_COLS = 16
D_FF = ((8 * D_MODEL // 3 + 255) // 256) * 256
CONV_W = 3
ROPE_BASE = 10000.0
EPS = 1e-6
NEG_INF = -1e30

W_QA = N_HEADS_A * HEAD_DIM
W_KA = N_KV_A * HEAD_DIM
W_B = N_HEADS_B * HEAD_DIM
SPLIT_WIDTHS = (W_QA, W_KA, W_KA, W_B, W_B, W_B, D_MODEL, D_MODEL)
SPLIT_POINTS = tuple(sum(SPLIT_WIDTHS[:i + 1]) for i in range(len(SPLIT_WIDTHS) - 1))
D_IN = sum(SPLIT_WIDTHS)

kernel_name = 'hybrid_dit_window_gqa_natten_convffn'


def rms_norm(x, g):
    xf = x.astype(jnp.float32)
    y = xf * lax.rsqrt(jnp.mean(jnp.square(xf), axis=-1, keepdims=True) + EPS)
    return (y * g.astype(jnp.float32)).astype(x.dtype)


def modulate(h, shift, scale):
    return h * (1 + scale) + shift


def axial_rope(t, row, col):
    half = t.shape[-1] // 2

    def rot(ta, pos):
        n = ta.shape[-1] // 2
        inv = ROPE_BASE ** (-jnp.arange(n, dtype=jnp.float32) / n)
        ang = pos.astype(jnp.float32)[:, None] * inv[None, :]
        cos = jnp.cos(ang)[None, :, None, :]
        sin = jnp.sin(ang)[None, :, None, :]
        t1 = ta[..., :n].astype(jnp.float32)
        t2 = ta[..., n:].astype(jnp.float32)
        return jnp.concatenate([t1 * cos - t2 * sin, t2 * cos + t1 * sin], axis=-1).astype(ta.dtype)

    return jnp.concatenate([rot(t[..., :half], row), rot(t[..., half:], col)], axis=-1)


def context_attention(qc, kc, vc, sink):
    b, l, kv, g, d = qc.shape
    s = jnp.einsum('bqkgd,bjkd->bkgqj', qc, kc, preferred_element_type=jnp.float32) * (d ** -0.5)
    if sink is not None:
        sink_col = jnp.broadcast_to(sink.astype(jnp.float32)[None, :, :, None, None], s.shape[:-1] + (1,))
        s = jnp.concatenate([s, sink_col], axis=-1)
    p = jax.nn.softmax(s, axis=-1)
    if sink is not None:
        p = p[..., :-1]
    o = jnp.einsum('bkgqj,bjkd->bqkgd', p.astype(vc.dtype), vc)
    return o.reshape(b, l, kv * g * d)


def window_gqa_latent(q, k, v, kc, vc, sink):
    b, s, _, d = q.shape
    nb = s // BLOCK
    scale = d ** -0.5
    qb = q.reshape(b, nb, BLOCK, N_KV_A, GQA_GROUP, d)

    def band(t):
        tp = jnp.pad(t, ((0, 0), (BLOCK, BLOCK), (0, 0), (0, 0))).reshape(b, nb + 2, BLOCK, N_KV_A, d)
        return jnp.concatenate([tp[:, :-2], tp[:, 1:-1], tp[:, 2:]], axis=2)

    kb, vb = band(k), band(v)
    s_loc = jnp.einsum('bnqkgd,bnjkd->bnkgqj', qb, kb, preferred_element_type=jnp.float32) * scale
    blk = jnp.arange(nb)[:, None, None]
    q_pos = blk * BLOCK + jnp.arange(BLOCK)[None, :, None]
    k_pos = (blk - 1) * BLOCK + jnp.arange(3 * BLOCK)[None, None, :]
    valid = (k_pos >= 0) & (k_pos < s) & (jnp.abs(k_pos - q_pos) <= WINDOW)
    s_loc = jnp.where(valid[None, :, None, None], s_loc, NEG_INF)
    s_ctx = jnp.einsum('bnqkgd,bjkd->bnkgqj', qb, kc, preferred_element_type=jnp.float32) * scale
    sink_col = jnp.broadcast_to(
        sink.reshape(N_KV_A, GQA_GROUP).astype(jnp.float32)[None, None, :, :, None, None],
        s_ctx.shape[:-1] + (1,))
    p = jax.nn.softmax(jnp.concatenate([s_loc, s_ctx, sink_col], axis=-1), axis=-1).astype(v.dtype)
    n_loc = 3 * BLOCK
    n_ctx = kc.shape[1]
    o = (jnp.einsum('bnkgqj,bnjkd->bnqkgd', p[..., :n_loc], vb)
         + jnp.einsum('bnkgqj,bjkd->bnqkgd', p[..., n_loc:n_loc + n_ctx], vc))
    return o.reshape(b, s, N_HEADS_A * d)


def neighbourhood_latent(q, k, v, kc, vc, rpb):
    b, s, h, d = q.shape
    rows = s // GRID_W
    kr = min(NB_ROWS, rows)
    scale = d ** -0.5
    r = jnp.arange(rows)
    row_idx = jnp.clip(r - kr // 2, 0, rows - kr)[:, None] + jnp.arange(kr)[None, :]
    cq = jnp.arange(GRID_W)
    col_start = jnp.clip(cq - NB_COLS // 2, 0, GRID_W - NB_COLS)
    col_valid = (cq[None, :] >= col_start[:, None]) & (cq[None, :] < col_start[:, None] + NB_COLS)
    qg = q.reshape(b, rows, GRID_W, h, d)
    kg = k.reshape(b, rows, GRID_W, h, d)[:, row_idx]
    vg = v.reshape(b, rows, GRID_W, h, d)[:, row_idx]
    s_nb = jnp.einsum('brqhd,brikhd->brhqik', qg, kg, preferred_element_type=jnp.float32) * scale
    dr = row_idx - r[:, None] + (NB_ROWS - 1)
    dc = jnp.clip(cq[None, :] - cq[:, None], -(NB_COLS - 1), NB_COLS - 1) + (NB_COLS - 1)
    bias = rpb[:, dr[:, :, None, None], dc[None, None, :, :]]
    bias = jnp.transpose(bias, (1, 0, 3, 2, 4)).astype(jnp.float32)
    s_nb = jnp.where(col_valid[:, None, :], s_nb + bias[None], NEG_INF)
    n_loc = kr * GRID_W
    s_nb = s_nb.reshape(b, rows, h, GRID_W, n_loc)
    s_ctx = jnp.einsum('brqhd,bjhd->brhqj', qg, kc, preferred_element_type=jnp.float32) * scale
    p = jax.nn.softmax(jnp.concatenate([s_nb, s_ctx], axis=-1), axis=-1).astype(v.dtype)
    p_nb = p[..., :n_loc].reshape(b, rows, h, GRID_W, kr, GRID_W)
    o = (jnp.einsum('brhqik,brikhd->brqhd', p_nb, vg)
         + jnp.einsum('brhqj,bjhd->brqhd', p[..., n_loc:], vc))
    return o.reshape(b, s, h * d)


def mixer_sublayer(h, hc, w_in, sink_a, rpb_b, w_br_a, w_br_b, w_o, need_ctx):
    b, s, _ = h.shape
    l = hc.shape[1]
    t = jnp.arange(s)
    row, col = t // GRID_W, t % GRID_W
    qa, ka, va, qb, kb, vb, ga, gb = jnp.split(h @ w_in, SPLIT_POINTS, axis=-1)
    qa_c, ka_c, va_c, qb_c, kb_c, vb_c, ga_c, gb_c = jnp.split(hc @ w_in, SPLIT_POINTS, axis=-1)

    qa = axial_rope(qa.reshape(b, s, N_HEADS_A, HEAD_DIM), row, col)
    ka = axial_rope(ka.reshape(b, s, N_KV_A, HEAD_DIM), row, col)
    va = va.reshape(b, s, N_KV_A, HEAD_DIM)
    ka_c = ka_c.reshape(b, l, N_KV_A, HEAD_DIM)
    va_c = va_c.reshape(b, l, N_KV_A, HEAD_DIM)
    qb = qb.reshape(b, s, N_HEADS_B, HEAD_DIM)
    kb = kb.reshape(b, s, N_HEADS_B, HEAD_DIM)
    vb = vb.reshape(b, s, N_HEADS_B, HEAD_DIM)
    kb_c = kb_c.reshape(b, l, N_HEADS_B, HEAD_DIM)
    vb_c = vb_c.reshape(b, l, N_HEADS_B, HEAD_DIM)

    oa = window_gqa_latent(qa, ka, va, ka_c, va_c, sink_a)
    ob = neighbourhood_latent(qb, kb, vb, kb_c, vb_c, rpb_b)
    y = (jax.nn.sigmoid(ga) * (oa @ w_br_a) + jax.nn.sigmoid(gb) * (ob @ w_br_b)) @ w_o

    yc = None
    if need_ctx:
        oa_c = context_attention(qa_c.reshape(b, l, N_KV_A, GQA_GROUP, HEAD_DIM), ka_c, va_c,
                                 sink_a.reshape(N_KV_A, GQA_GROUP))
        ob_c = context_attention(qb_c.reshape(b, l, N_HEADS_B, 1, HEAD_DIM), kb_c, vb_c, None)
        yc = (jax.nn.sigmoid(ga_c) * (oa_c @ w_br_a) + jax.nn.sigmoid(gb_c) * (ob_c @ w_br_b)) @ w_o
    return y, yc


def depthwise_conv(u, w, bias):
    pad = CONV_W // 2
    t = u.shape[1]
    up = jnp.pad(u, ((0, 0), (pad, pad), (0, 0)))
    out = bias
    for j in range(CONV_W):
        out = out + up[:, j:j + t] * w[j]
    return out


def conv_ffn(h, w_up, conv_w, conv_b, w_down):
    u = depthwise_conv(h @ w_up, conv_w, conv_b)
    a, g = jnp.split(u, 2, axis=-1)
    return (jax.nn.silu(g) * a) @ w_down


def setup_inputs(seed: int = 0) -> dict:
    key = jax.random.key(seed)
    ks = jax.random.split(key, 20)

    def nrm(k, shape, scale):
        return jax.random.normal(k, shape, jnp.float32) * scale

    def gain(k):
        return 1.0 + nrm(k, (DEPTH, D_MODEL), 0.05)

    return {
        'x': nrm(ks[0], (BATCH, SEQ, D_MODEL), 1.0),
        'c': nrm(ks[1], (BATCH, D_MODEL), 1.0),
        'ctx': nrm(ks[2], (BATCH, CTX_LEN, D_MODEL), 1.0),
        'c_ctx': nrm(ks[3], (D_MODEL,), 1.0),
        'w_mod': nrm(ks[4], (DEPTH, D_MODEL, 6 * D_MODEL), D_MODEL ** -0.5),
        'b_mod': nrm(ks[5], (DEPTH, 6 * D_MODEL), 0.01),
        'g_attn_pre': gain(ks[6]),
        'g_attn_post': gain(ks[7]),
        'g_ffn_pre': gain(ks[8]),
        'g_ffn_post': gain(ks[9]),
        'w_in': nrm(ks[10], (DEPTH, D_MODEL, D_IN), D_MODEL ** -0.5),
        'sink_a': nrm(ks[11], (DEPTH, N_HEADS_A), 1.0),
        'rpb_b': nrm(ks[12], (DEPTH, N_HEADS_B, 2 * NB_ROWS - 1, 2 * NB_COLS - 1), 0.5),
        'w_br_a': nrm(ks[13], (DEPTH, W_QA, D_MODEL), W_QA ** -0.5),
        'w_br_b': nrm(ks[14], (DEPTH, W_B, D_MODEL), W_B ** -0.5),
        'w_o': nrm(ks[15], (DEPTH, D_MODEL, D_MODEL), D_MODEL ** -0.5),
        'w_up': nrm(ks[16], (DEPTH, D_MODEL, 2 * D_FF), D_MODEL ** -0.5),
        'conv_w': nrm(ks[17], (DEPTH, CONV_W, 2 * D_FF), CONV_W ** -0.5),
        'conv_b': nrm(ks[18], (DEPTH, 2 * D_FF), 0.01),
        'w_down': nrm(ks[19], (DEPTH, D_FF, D_MODEL), D_FF ** -0.5),
    }


def reference(x, c, ctx, c_ctx, w_mod, b_mod, g_attn_pre, g_attn_post, g_ffn_pre, g_ffn_post,
              w_in, sink_a, rpb_b, w_br_a, w_br_b, w_o, w_up, conv_w, conv_b, w_down):
    for l in range(DEPTH):
        need_ctx = l < DEPTH - 1
        mod = jax.nn.silu(c) @ w_mod[l] + b_mod[l]
        mod_c = jax.nn.silu(c_ctx) @ w_mod[l] + b_mod[l]
        sh1, sc1, gt1, sh2, sc2, gt2 = [m[:, None, :] for m in jnp.split(mod, 6, axis=-1)]
        csh1, csc1, cgt1, csh2, csc2, cgt2 = jnp.split(mod_c, 6, axis=-1)

        h = modulate(rms_norm(x, g_attn_pre[l]), sh1, sc1)
        hc = modulate(rms_norm(ctx, g_attn_pre[l]), csh1, csc1)
        y, yc = mixer_sublayer(h, hc, w_in[l], sink_a[l], rpb_b[l], w_br_a[l], w_br_b[l], w_o[l], need_ctx)
        x = x + gt1 * rms_norm(y, g_attn_post[l])

        h = modulate(rms_norm(x, g_ffn_pre[l]), sh2, sc2)
        x = x + gt2 * rms_norm(conv_ffn(h, w_up[l], conv_w[l], conv_b[l], w_down[l]), g_ffn_post[l])

        if need_ctx:
            ctx = ctx + cgt1 * rms_norm(yc, g_attn_post[l])
            hc = modulate(rms_norm(ctx, g_ffn_pre[l]), csh2, csc2)
            ctx = ctx + cgt2 * rms_norm(conv_ffn(hc, w_up[l], conv_w[l], conv_b[l], w_down[l]), g_ffn_post[l])
    return x
```

```python
import numpy as np
from contextlib import ExitStack
import concourse.bass as bass
import concourse.mybir as mybir
from concourse.bass_utils import run_bass_kernel_spmd

F32 = mybir.dt.float32
BF16 = mybir.dt.bfloat16
AF = mybir.ActivationFunctionType
ALU = mybir.AluOpType

D = 2048
DFF = 5632
NF = DFF // 128
HD = 128
GRID_W = 64
CTX = 256
EPS = 1e-6
ROPE_BASE = 10000.0
NB_ROWS, NB_COLS = 8, 16
SCALE = HD ** -0.5


class Sem:
    def __init__(self, handle, name):
        self.h = handle
        self.name = name
        self.count = 0


class Buf:
    __slots__ = ("name", "w", "r")

    def __init__(self, name=""):
        self.name = name
        self.w = None
        self.r = []


class Sched:
    ENGS = ("pe", "act", "dve", "pool", "sp")

    def __init__(self, nc, stack):
        self.nc = nc
        self.stack = stack
        self.prog = {e: [] for e in self.ENGS}
        self.esem = {e: self.new_sem("e_" + e) for e in self.ENGS}
        self.seen = {e: {} for e in self.ENGS}
        self.dsems = {}

    def new_sem(self, name):
        h = self.stack.enter_context(self.nc.semaphore(name))
        return Sem(h, name)

    def dsem(self, name):
        if name not in self.dsems:
            self.dsems[name] = self.new_sem("d_" + name)
        return self.dsems[name]

    def _deps(self, reads, writes):
        deps = []
        for b in reads:
            if b.w is not None:
                deps.append(b.w)
        for b in writes:
            if b.w is not None:
                deps.append(b.w)
            deps.extend(b.r)
        return deps

    def _emit_waits(self, eng, deps):
        seen = self.seen[eng]
        need = {}
        for (s, v) in deps:
            if seen.get(s, 0) >= v:
                continue
            if need.get(s, 0) < v:
                need[s] = v
        for s, v in need.items():
            seen[s] = v
            self.prog[eng].append(lambda e, s=s, v=v: e.wait_ge(s.h, v))

    def _commit(self, tok, reads, writes):
        for b in writes:
            b.w = tok
            b.r = []
        for b in reads:
            if b in writes:
                continue
            b.r.append(tok)
            if len(b.r) > 16:
                mx = {}
                for (s, v) in b.r:
                    if mx.get(s, 0) < v:
                        mx[s] = v
                b.r = list(mx.items())

    def op(self, eng, fn, reads=(), writes=()):
        reads = list(reads)
        writes = list(writes)
        self._emit_waits(eng, self._deps(reads, writes))
        s = self.esem[eng]
        s.count += 1
        tok = (s, s.count)
        self.prog[eng].append(lambda e, fn=fn, s=s: fn(e).then_inc(s.h, 1))
        self.seen[eng][s] = max(self.seen[eng].get(s, 0), 0)
        self._commit(tok, reads, writes)
        return tok

    def dma(self, eng, semname, fn, reads=(), writes=()):
        sem = self.dsem(semname)
        reads = list(reads)
        writes = list(writes)
        deps = self._deps(reads, writes)
        if sem.count > 0:
            deps.append((sem, sem.count))
        self._emit_waits(eng, deps)
        sem.count += 16
        tok = (sem, sem.count)
        self.prog[eng].append(lambda e, fn=fn, sem=sem: fn(e).then_inc(sem.h, 16))
        self._commit(tok, reads, writes)
        return tok

    def barrier(self):
        toks = [(s, s.count) for s in self.esem.values() if s.count > 0]
        toks += [(s, s.count) for s in self.dsems.values() if s.count > 0]
        for e in self.ENGS:
            self._emit_waits(e, toks)

    def emit(self):
        nc = self.nc
        prog = self.prog
        with nc.Block() as block:
            @block.tensor
            def _(e):
                for f in prog["pe"]:
                    f(e)

            @block.scalar
            def _(e):
                for f in prog["act"]:
                    f(e)

            @block.vector
            def _(e):
                for f in prog["dve"]:
                    f(e)

            @block.gpsimd
            def _(e):
                for f in prog["pool"]:
                    f(e)

            @block.sync
            def _(e):
                for f in prog["sp"]:
                    f(e)
        self.prog = {e: [] for e in self.ENGS}


def split_even(n, mx, q=1):
    k = -(-n // mx)
    base = (n // k) // q * q
    sizes = [base] * k
    rem = n - base * k
    i = 0
    while rem > 0:
        sizes[i] += q
        rem -= q
        i += 1
    offs = np.cumsum([0] + sizes[:-1]).tolist()
    return list(zip(offs, sizes))


def build(NOWN):
    NL = NOWN + 6
    NM = NOWN + 2
    NLT, NMT, NOT = NL * 128, NM * 128, NOWN * 128
    nc = bass.Bass("TRN2", target_bir_lowering=False)

    def din(name, shape, dt=F32):
        return nc.dram_tensor(name, list(shape), dt, kind="ExternalInput").ap()

    x_loc = din("x_loc", [NLT, D])
    ctx_in = din("ctx", [CTX, D])
    cT_in = din("cT", [128, 16, 2])
    bmT_in = din("bmT", [128, 96])
    gv_in = din("gv", [128, 4, 16])
    w_mod = din("w_mod", [D, 6 * D])
    w_in = din("w_in", [D, 8704])
    w_br_a = din("w_br_a", [1024, D])
    w_br_b = din("w_br_b", [1024, D])
    w_o = din("w_o", [D, D])
    w_up = din("w_up", [D, 2 * DFF])
    w_down = din("w_down", [DFF, D])
    cwT_in = din("cwT", [128, 88, 3])
    cbT_in = din("cbT", [128, 88])
    sinkb_in = din("sinkb", [128, 8, 128])
    Gtab_in = din("Gtab", [128, 8, 7, 128])
    Mgen_in = din("Mgen", [128, 5, 128])
    Msp_in = din("Msp", [128, 4, 7, 128])
    MA_in = din("MA", [128, 3, 1024])
    ropec = din("ropec", [128, NLT])
    ropes = din("ropes", [128, NLT])
    perm_in = din("perm", [128, 128])
    ident_in = din("ident", [128, 128])
    edge_in = din("edge", [128, 2])
    out = nc.dram_tensor("out", [NOT, D], F32, kind="ExternalOutput").ap()
    ST = nc.dram_tensor("ST", [68, 128, NLT], BF16).ap()
    STc = nc.dram_tensor("STc", [20, 128, CTX], BF16).ap()
    OTs = nc.dram_tensor("OTs", [16, 128, NMT], BF16).ap()
    ZT = nc.dram_tensor("ZT", [16, 128, NMT], BF16).ap()
    X1 = nc.dram_tensor("X1", [NMT, D], F32).ap()

    with ExitStack() as top:
        S = Sched(nc, top)

        def sbt(stack, name, shape, dt):
            return stack.enter_context(nc.sbuf_tensor("s_" + name, list(shape), dt))

        def pst(stack, name, shape, dt=F32):
            return stack.enter_context(nc.psum_tensor("p_" + name, list(shape), dt))

        ident = sbt(top, "ident", [128, 128], F32)
        identb = sbt(top, "identb", [128, 128], BF16)
        onesf = sbt(top, "onesf", [128, 128], F32)
        onesb = sbt(top, "onesb", [128, 128], BF16)
        epst = sbt(top, "epst", [128, 1], F32)
        vec = sbt(top, "vec", [128, 8, 16], F32)
        G1 = sbt(top, "G1", [128, D], F32)
        G2 = sbt(top, "G2", [128, D], F32)
        b_const = Buf("const")
        b_vec = Buf("vec")
        b_G = Buf("G")
        S.dma("sp", "misc", lambda e: e.dma_start(out=ident[:], in_=ident_in), writes=[b_const])
        S.op("dve", lambda e: e.tensor_copy(out=identb[:], in_=ident[:]), reads=[b_const], writes=[b_const])
        S.op("dve", lambda e: e.memset(onesf[:], 1.0), writes=[b_const])
        S.op("dve", lambda e: e.memset(onesb[:], 1.0), writes=[b_const])
        S.op("dve", lambda e: e.memset(epst[:], EPS), writes=[b_const])

        class LN:
            def __init__(self, stack, tag, nbuf=2):
                self.nbuf = nbuf
                self.xt = [sbt(stack, f"ln_xt{tag}{i}", [128, D], F32) for i in range(nbuf)]
                self.xn = [sbt(stack, f"ln_xn{tag}{i}", [128, D], BF16) for i in range(nbuf)]
                self.junk = sbt(stack, f"ln_junk{tag}", [128, D], BF16)
                self.ss = [sbt(stack, f"ln_ss{tag}{i}", [128, 4], F32) for i in range(nbuf)]
                self.pT = [pst(stack, f"ln_pT{tag}{i}", [128, 16, 128], BF16) for i in range(nbuf)]
                self.b_xt = [Buf() for _ in range(nbuf)]
                self.b_xn = [Buf() for _ in range(nbuf)]
                self.b_ss = [Buf() for _ in range(nbuf)]
                self.b_pT = [Buf() for _ in range(nbuf)]
                self.b_junk = Buf()
                self.i = 0
                self.tag = tag

            def load(self, src, rows):
                i = self.i % self.nbuf
                xt = self.xt[i]
                S.dma("sp", f"lnx{self.tag}{i}", lambda e: e.dma_start(out=xt[0:rows, :], in_=src), writes=[self.b_xt[i]])

            def run(self, rows, sc_ap, sh_ap, dst_fn, b_dst):
                i = self.i % self.nbuf
                self.i += 1
                xt, xn, ss, pT, junk = self.xt[i], self.xn[i], self.ss[i], self.pT[i], self.junk
                bx, bn, bs, bp = self.b_xt[i], self.b_xn[i], self.b_ss[i], self.b_pT[i]
                S.op("dve", lambda e: e.memset(ss[:], 0.0), writes=[bs])
                S.op("act", lambda e: e.activation(out=junk[0:rows, :], in_=xt[0:rows, :], func=AF.Square,
                                                   accum_out=ss[0:rows, 0:1]), reads=[bx], writes=[self.b_junk, bs])
                S.op("act", lambda e: e.activation(out=ss[0:rows, 1:2], in_=ss[0:rows, 0:1], func=AF.Sqrt,
                                                   scale=1.0 / D, bias=epst[0:rows, 0:1]), reads=[bs, b_const], writes=[bs])
                S.op("dve", lambda e: e.reciprocal(out=ss[0:rows, 2:3], in_=ss[0:rows, 1:2]), reads=[bs], writes=[bs])
                S.op("dve", lambda e: e.tensor_scalar(out=xn[0:rows, :], in0=xt[0:rows, :], scalar1=ss[0:rows, 2:3],
                                                      scalar2=None, op0=ALU.mult), reads=[bx, bs], writes=[bn])

                def tr(e):
                    r = None
                    for k in range(16):
                        r = e.transpose(out=pT[:, k, 0:rows], in_=xn[0:rows, k * 128:(k + 1) * 128],
                                        identity=identb[0:rows, 0:rows])
                    return r
                S.op("pe", tr, reads=[bn, b_const], writes=[bp])

                def ev(e):
                    r = None
                    for k in range(16):
                        r = e.activation(out=dst_fn(k), in_=pT[:, k, 0:rows], func=AF.Identity,
                                         scale=sc_ap[:, k:k + 1], bias=sh_ap[:, k:k + 1])
                    return r
                S.op("act", ev, reads=[bp, b_vec], writes=[b_dst])

        class Epi:
            def __init__(self, stack, tag, nbuf=2):
                self.nbuf = nbuf
                self.junk = sbt(stack, f"ep_junk{tag}", [128, 1024], BF16)
                self.ss = [sbt(stack, f"ep_ss{tag}{i}", [128, 8], F32) for i in range(nbuf)]
                self.xr = [sbt(stack, f"ep_xr{tag}{i}", [128, D], F32) for i in range(nbuf)]
                self.xo = [sbt(stack, f"ep_xo{tag}{i}", [128, D], F32) for i in range(nbuf)]
                self.b_ss = [Buf() for _ in range(nbuf)]
                self.b_junk = Buf()
                self.b_xr = [Buf() for _ in range(nbuf)]
                self.b_xo = [Buf() for _ in range(nbuf)]
                self.i = 0
                self.tag = tag

            def load_res(self, src):
                i = self.i % self.nbuf
                xr = self.xr[i]
                S.dma("sp", f"epr{self.tag}{i}", lambda e: e.dma_start(out=xr[:], in_=src), writes=[self.b_xr[i]])

            def run(self, pYs, b_pYs, Gt, dst):
                i = self.i % self.nbuf
                self.i += 1
                ss, xr, xo, junk = self.ss[i], self.xr[i], self.xo[i], self.junk
                bs = self.b_ss[i]
                S.op("dve", lambda e: e.memset(ss[:], 0.0), writes=[bs])
                for h in range(2):
                    S.op("act", lambda e, h=h: e.activation(out=junk[:], in_=pYs[h], func=AF.Square,
                                                            accum_out=ss[:, h:h + 1]), reads=[b_pYs[h]], writes=[self.b_junk, bs])
                S.op("dve", lambda e: e.tensor_tensor(out=ss[:, 2:3], in0=ss[:, 0:1], in1=ss[:, 1:2], op=ALU.add), reads=[bs], writes=[bs])
                S.op("act", lambda e: e.activation(out=ss[:, 3:4], in_=ss[:, 2:3], func=AF.Sqrt, scale=1.0 / D, bias=epst[:, 0:1]),
                     reads=[bs, b_const], writes=[bs])
                S.op("dve", lambda e: e.reciprocal(out=ss[:, 4:5], in_=ss[:, 3:4]), reads=[bs], writes=[bs])
                for h in range(2):
                    S.op("dve", lambda e, h=h: e.scalar_tensor_tensor(out=xo[:, h * 1024:(h + 1) * 1024], in0=pYs[h], scalar=ss[:, 4:5],
                                                                      in1=Gt[:, h * 1024:(h + 1) * 1024], op0=ALU.mult, op1=ALU.mult),
                         reads=[b_pYs[h], bs, b_G], writes=[self.b_xo[i]])
                S.op("dve", lambda e: e.tensor_tensor(out=xo[:], in0=xo[:], in1=xr[:], op=ALU.add),
                     reads=[self.b_xr[i]], writes=[self.b_xo[i]])
                S.dma("sp", f"epo{self.tag}{i}", lambda e: e.dma_start(out=dst, in_=xo[:]), reads=[self.b_xo[i]])

        with ExitStack() as ph:
            cT = sbt(ph, "cT", [128, 16, 2], F32)
            cS = sbt(ph, "cS", [128, 16, 2], BF16)
            bmT = sbt(ph, "bmT", [128, 96], F32)
            gv = sbt(ph, "gv", [128, 4, 16], F32)
            modT = sbt(ph, "modT", [128, 96, 2], F32)
            diag = [sbt(ph, f"diag{i}", [128, 128], F32) for i in range(2)]
            wm = [sbt(ph, f"wm{i}", [128, 16, 512], BF16) for i in range(3)]
            pM = pst(ph, "pM", [128, 96, 2], F32)
            pG = [pst(ph, f"pG{i}", [128, 512], F32) for i in range(2)]
            b_c, b_cS, b_bm, b_gv, b_mod, b_pM = [Buf() for _ in range(6)]
            b_wm = [Buf() for _ in range(3)]
            b_diag = [Buf() for _ in range(2)]
            b_pG = [Buf() for _ in range(2)]
            S.dma("sp", "misc", lambda e: e.dma_start(out=cT[:], in_=cT_in), writes=[b_c])
            S.dma("sp", "misc", lambda e: e.dma_start(out=bmT[:], in_=bmT_in), writes=[b_bm])
            S.dma("sp", "misc", lambda e: e.dma_start(out=gv[:], in_=gv_in), writes=[b_gv])
            S.op("act", lambda e: e.activation(out=cS[:], in_=cT[:], func=AF.Silu), reads=[b_c], writes=[b_cS])
            for n in range(24):
                wt = wm[n % 3]
                S.dma("pool", f"w{n % 3}", lambda e, wt=wt, n=n: e.dma_start(
                    out=wt[:], in_=w_mod[:, n * 512:(n + 1) * 512].rearrange("(k p) n -> p k n", p=128)), writes=[b_wm[n % 3]])

                def mm(e, wt=wt, n=n):
                    r = None
                    for m in range(4):
                        for k in range(16):
                            r = e.matmul(pM[:, n * 4 + m, :], lhsT=wt[:, k, m * 128:(m + 1) * 128], rhs=cS[:, k, :],
                                         start=(k == 0), stop=(k == 15))
                    return r
                S.op("pe", mm, reads=[b_wm[n % 3], b_cS], writes=[b_pM])
            for j in range(2):
                S.op("dve", lambda e, j=j: e.tensor_tensor(out=modT[:, :, j], in0=pM[:, :, j], in1=bmT[:], op=ALU.add),
                     reads=[b_pM, b_bm], writes=[b_mod])
            def mk_scale(dst, seg, col, g):
                S.op("dve", lambda e: e.scalar_tensor_tensor(out=vec[:, dst, :], in0=modT[:, seg * 16:(seg + 1) * 16, col], scalar=1.0,
                                                             in1=gv[:, g, :], op0=ALU.add, op1=ALU.mult), reads=[b_mod, b_gv], writes=[b_vec])

            def mk_copy(dst, seg, col):
                S.op("dve", lambda e: e.tensor_copy(out=vec[:, dst, :], in_=modT[:, seg * 16:(seg + 1) * 16, col]), reads=[b_mod], writes=[b_vec])

            def mk_mul(dst, seg, col, g):
                S.op("dve", lambda e: e.tensor_tensor(out=vec[:, dst, :], in0=modT[:, seg * 16:(seg + 1) * 16, col], in1=gv[:, g, :], op=ALU.mult),
                     reads=[b_mod, b_gv], writes=[b_vec])
            mk_scale(0, 1, 0, 0)
            mk_copy(1, 0, 0)
            mk_scale(2, 1, 1, 0)
            mk_copy(3, 0, 1)
            mk_scale(4, 4, 0, 2)
            mk_copy(5, 3, 0)
            mk_mul(6, 2, 0, 1)
            mk_mul(7, 5, 0, 3)
            cnt = 0
            for gi, Gt in ((6, G1), (7, G2)):
                for k in range(16):
                    dg = diag[cnt % 2]
                    bd = b_diag[cnt % 2]
                    pg = pG[cnt % 2]
                    bp = b_pG[cnt % 2]
                    S.op("dve", lambda e, dg=dg, gi=gi, k=k: e.tensor_scalar(out=dg[:], in0=ident[:], scalar1=vec[:, gi, k:k + 1], scalar2=None,
                                                                             op0=ALU.mult), reads=[b_const, b_vec], writes=[bd])
                    S.op("pe", lambda e, dg=dg, pg=pg: e.matmul(pg[:, 0:128], lhsT=onesf[:], rhs=dg[:], start=True, stop=True),
                         reads=[bd, b_const], writes=[bp])
                    S.op("act", lambda e, Gt=Gt, pg=pg, k=k: e.activation(out=Gt[:, k * 128:(k + 1) * 128], in_=pg[:, 0:128], func=AF.Copy),
                         reads=[bp], writes=[b_G])
                    cnt += 1
            S.barrier()
            S.emit()

        with ExitStack() as ph:
            sts = split_even(NL, 13)
            MAXT = max(nb for _, nb in sts) * 128
            hT = sbt(ph, "hT", [128, 16, MAXT + CTX], BF16)
            cosT = sbt(ph, "cosT", [128, MAXT], F32)
            sinT = sbt(ph, "sinT", [128, MAXT], F32)
            permf = sbt(ph, "permf", [128, 128], F32)
            NW = 3
            wi = [sbt(ph, f"wi{i}", [128, 16, 512], BF16) for i in range(NW)]
            stg = [sbt(ph, f"stg{i}", [128, MAXT], BF16) for i in range(3)]
            stgc = [sbt(ph, f"stgc{i}", [128, CTX], BF16) for i in range(2)]
            q32 = [sbt(ph, f"q32{i}", [128, 512], F32) for i in range(2)]
            t1 = [sbt(ph, f"t1{i}", [128, 512], F32) for i in range(2)]
            t2 = [sbt(ph, f"t2{i}", [128, 512], F32) for i in range(2)]
            ln = LN(ph, "a")
            pP = [pst(ph, f"pP{i}", [128, 512], F32) for i in range(3)]
            pR = pst(ph, "pR", [128, 512], F32)
            b_hT, b_cs, b_perm, b_pR = Buf(), Buf(), Buf(), Buf()
            b_wi = [Buf() for _ in range(NW)]
            b_stg = [Buf() for _ in range(3)]
            b_stgc = [Buf() for _ in range(2)]
            b_q32 = [Buf() for _ in range(2)]
            b_t1 = [Buf() for _ in range(2)]
            b_t2 = [Buf() for _ in range(2)]
            b_pP = [Buf() for _ in range(3)]
            b_ST = Buf("ST")
            S.dma("sp", "misc", lambda e: e.dma_start(out=permf[:], in_=perm_in), writes=[b_perm])
            ctxmap = {8: 0, 9: 1, 10: 2, 11: 3}
            for i in range(8):
                ctxmap[20 + i] = 4 + i
                ctxmap[28 + i] = 12 + i
            wcnt = 0
            pcnt = 0
            rcnt = 0
            scnt = 0
            sccnt = 0
            for sti, (b0, nb) in enumerate(sts):
                ntok = nb * 128
                tok0 = b0 * 128
                S.dma("sp", "cos", lambda e, tok0=tok0, ntok=ntok: e.dma_start(out=cosT[:, 0:ntok], in_=ropec[:, tok0:tok0 + ntok]), writes=[b_cs])
                S.dma("sp", "sin", lambda e, tok0=tok0, ntok=ntok: e.dma_start(out=sinT[:, 0:ntok], in_=ropes[:, tok0:tok0 + ntok]), writes=[b_cs])
                ln.load(x_loc[tok0:tok0 + 128, :], 128)
                for bi in range(nb):
                    if bi + 1 < nb:
                        pass
                    ln.run(128, vec[:, 0, :], vec[:, 1, :], lambda k, bi=bi: hT[:, k, bi * 128:(bi + 1) * 128], b_hT)
                    if bi + 1 < nb:
                        ln.load(x_loc[tok0 + (bi + 1) * 128:tok0 + (bi + 2) * 128, :], 128)
                with_ctx = (sti == 0)
                if with_ctx:
                    for cb in range(2):
                        ln.load(ctx_in[cb * 128:(cb + 1) * 128, :], 128)
                        ln.run(128, vec[:, 2, :], vec[:, 3, :], lambda k, cb=cb: hT[:, k, MAXT + cb * 128:MAXT + (cb + 1) * 128], b_hT)
                groups = split_even(ntok, 512, 128)
                for wc in range(17):
                    wt = wi[wcnt % NW]
                    bw = b_wi[wcnt % NW]
                    S.dma("pool", f"w{wcnt % NW}", lambda e, wt=wt, wc=wc: e.dma_start(
                        out=wt[:], in_=w_in[:, wc * 512:(wc + 1) * 512].rearrange("(k p) n -> p k n", p=128)), writes=[bw])
                    wcnt += 1
                    for m in range(4):
                        c = wc * 4 + m
                        is_rope = c < 10
                        is_gate = c >= 36
                        sg = stg[scnt % 3]
                        bsg = b_stg[scnt % 3]
                        sgname = f"st{scnt % 3}"
                        scnt += 1
                        pending = []

                        def flush():
                            for (g0, gn, ri) in pending:
                                S.op("pe", lambda e, ri=ri, gn=gn: e.matmul(pR[:, 0:gn], lhsT=permf[:], rhs=q32[ri][:, 0:gn], start=True, stop=True),
                                     reads=[b_q32[ri], b_perm], writes=[b_pR])
                                S.op("dve", lambda e, ri=ri, g0=g0, gn=gn: e.tensor_tensor(out=t1[ri][:, 0:gn], in0=q32[ri][:, 0:gn], in1=cosT[:, g0:g0 + gn], op=ALU.mult),
                                     reads=[b_q32[ri], b_cs], writes=[b_t1[ri]])
                                S.op("dve", lambda e, ri=ri, g0=g0, gn=gn: e.tensor_tensor(out=t2[ri][:, 0:gn], in0=pR[:, 0:gn], in1=sinT[:, g0:g0 + gn], op=ALU.mult),
                                     reads=[b_pR, b_cs], writes=[b_t2[ri]])
                                S.op("dve", lambda e, ri=ri, g0=g0, gn=gn, sg=sg: e.tensor_tensor(out=sg[:, g0:g0 + gn], in0=t1[ri][:, 0:gn], in1=t2[ri][:, 0:gn], op=ALU.add),
                                     reads=[b_t1[ri], b_t2[ri]], writes=[bsg])
                            pending.clear()

                        for (g0, gn) in groups:
                            pp = pP[pcnt % 3]
                            bpp = b_pP[pcnt % 3]
                            pcnt += 1

                            def mm(e, pp=pp, wt=wt, m=m, g0=g0, gn=gn):
                                r = None
                                for k in range(16):
                                    r = e.matmul(pp[:, 0:gn], lhsT=wt[:, k, m * 128:(m + 1) * 128], rhs=hT[:, k, g0:g0 + gn],
                                                 start=(k == 0), stop=(k == 15))
                                return r
                            S.op("pe", mm, reads=[bw, b_hT], writes=[bpp])
                            flush()
                            if is_rope:
                                ri = rcnt % 2
                                rcnt += 1
                                S.op("act", lambda e, pp=pp, ri=ri, gn=gn: e.activation(out=q32[ri][:, 0:gn], in_=pp[:, 0:gn], func=AF.Copy),
                                     reads=[bpp], writes=[b_q32[ri]])
                                pending.append((g0, gn, ri))
                            elif is_gate:
                                S.op("act", lambda e, pp=pp, sg=sg, g0=g0, gn=gn: e.activation(out=sg[:, g0:g0 + gn], in_=pp[:, 0:gn], func=AF.Sigmoid),
                                     reads=[bpp], writes=[bsg])
                            else:
                                S.op("act", lambda e, pp=pp, sg=sg, g0=g0, gn=gn: e.activation(out=sg[:, g0:g0 + gn], in_=pp[:, 0:gn], func=AF.Copy),
                                     reads=[bpp], writes=[bsg])
                        flush()
                        S.dma("sp", sgname, lambda e, sg=sg, c=c, tok0=tok0, ntok=ntok: e.dma_start(out=ST[c, :, tok0:tok0 + ntok], in_=sg[:, 0:ntok]),
                              reads=[bsg], writes=[b_ST])
                        if with_ctx and c in ctxmap:
                            pp = pP[pcnt % 3]
                            bpp = b_pP[pcnt % 3]
                            pcnt += 1
                            sc_ = stgc[sccnt % 2]
                            bsc = b_stgc[sccnt % 2]
                            scname = f"sc{sccnt % 2}"
                            sccnt += 1

                            def mmc(e, pp=pp, wt=wt, m=m):
                                r = None
                                for k in range(16):
                                    r = e.matmul(pp[:, 0:CTX], lhsT=wt[:, k, m * 128:(m + 1) * 128], rhs=hT[:, k, MAXT:MAXT + CTX],
                                                 start=(k == 0), stop=(k == 15))
                                return r
                            S.op("pe", mmc, reads=[bw, b_hT], writes=[bpp])
                            S.op("act", lambda e, pp=pp, sc_=sc_: e.activation(out=sc_[:], in_=pp[:, 0:CTX], func=AF.Copy), reads=[bpp], writes=[bsc])
                            S.dma("sp", scname, lambda e, sc_=sc_, ci=ctxmap[c]: e.dma_start(out=STc[ci, :, :], in_=sc_[:]), reads=[bsc], writes=[b_ST])
            S.barrier()
            S.emit()

        with ExitStack() as ph:
            expG = sbt(ph, "expG", [128, 8, 7, 128], F32)
            Egen = sbt(ph, "Egen", [128, 8, 5, 128], F32)
            Et = [sbt(ph, f"Et{i}", [128, 7, 128], F32) for i in range(2)]
            Mgen = sbt(ph, "Mgen", [128, 5, 128], F32)
            Msp = sbt(ph, "Msp", [128, 4, 7, 128], F32)
            MAp = sbt(ph, "MAp", [128, 3, 1024], F32)
            sinkE = sbt(ph, "sinkE", [128, 8, 128], F32)
            Kr = [sbt(ph, f"Kr{i}", [128, 10, 128], BF16) for i in range(8)]
            Vr = [sbt(ph, f"Vr{i}", [128, 10, 128], BF16) for i in range(8)]
            VTs = [sbt(ph, f"VTs{i}", [128, 10, 128], BF16) for i in range(2)]
            Kc = sbt(ph, "Kc", [128, 10, CTX], BF16)
            Vc = sbt(ph, "Vc", [128, 2, 10, 128], BF16)
            Qt = [sbt(ph, f"Qt{i}", [128, 16, 128], BF16) for i in range(2)]
            S32 = [sbt(ph, f"S32{i}", [128, 1024], F32) for i in range(2)]
            Pb = [sbt(ph, f"Pb{i}", [128, 2560], BF16) for i in range(2)]
            rec = [sbt(ph, f"rec{i}", [128, 512], F32) for i in range(2)]
            OTb = [sbt(ph, f"OTb{i}", [128, 16, 128], BF16) for i in range(2)]
            pV = pst(ph, "pV", [128, 10, 128], BF16)
            pS = [pst(ph, f"pS{i}", [128, 1024], F32) for i in range(2)]
            pOD = [pst(ph, f"pOD{i}", [128, 512], F32) for i in range(2)]
            b_tab, b_Egen, b_sink, b_ctxkv, b_pV = [Buf() for _ in range(5)]
            b_Et = [Buf() for _ in range(2)]
            b_Kr = [Buf() for _ in range(8)]
            b_Vr = [Buf() for _ in range(8)]
            b_VTs = [Buf() for _ in range(2)]
            b_Qt = [Buf() for _ in range(2)]
            b_S32 = [Buf() for _ in range(2)]
            b_Pb = [Buf() for _ in range(2)]
            b_rec = [Buf() for _ in range(2)]
            b_OTb = [Buf() for _ in range(2)]
            b_pS = [Buf() for _ in range(2)]
            b_pOD = [Buf() for _ in range(2)]
            b_OTs = Buf()
            S.dma("sp", "misc", lambda e: e.dma_start(out=expG[:], in_=Gtab_in), writes=[b_tab])
            S.dma("sp", "misc", lambda e: e.dma_start(out=Mgen[:], in_=Mgen_in), writes=[b_tab])
            S.dma("sp", "misc", lambda e: e.dma_start(out=Msp[:], in_=Msp_in), writes=[b_tab])
            S.dma("sp", "misc", lambda e: e.dma_start(out=MAp[:], in_=MA_in), writes=[b_tab])
            S.dma("sp", "misc", lambda e: e.dma_start(out=sinkE[:], in_=sinkb_in), writes=[b_sink])
            S.op("act", lambda e: e.activation(out=expG[:].rearrange("p a b c -> p (a b c)"), in_=expG[:].rearrange("p a b c -> p (a b c)"), func=AF.Exp),
                 reads=[b_tab], writes=[b_tab])
            S.op("act", lambda e: e.activation(out=sinkE[:].rearrange("p a b -> p (a b)"), in_=sinkE[:].rearrange("p a b -> p (a b)"), func=AF.Exp),
                 reads=[b_sink], writes=[b_sink])
            for h in range(8):
                S.op("dve", lambda e, h=h: e.tensor_tensor(out=Egen[:, h, :, :], in0=expG[:, h, 1:6, :], in1=Mgen[:], op=ALU.mult),
                     reads=[b_tab], writes=[b_Egen])
            S.dma("sp", "misc", lambda e: e.dma_start(out=Kc[:, 0:2, :], in_=STc[0:2].rearrange("c p t -> p c t")), reads=[b_ST], writes=[b_ctxkv])
            S.dma("sp", "misc", lambda e: e.dma_start(out=Kc[:, 2:10, :], in_=STc[4:12].rearrange("c p t -> p c t")), reads=[b_ST], writes=[b_ctxkv])
            for cb in range(2):
                vt = VTs[cb]
                S.dma("sp", f"vt{cb}", lambda e, vt=vt, cb=cb: e.dma_start(out=vt[:, 0:2, :], in_=STc[2:4, :, cb * 128:(cb + 1) * 128].rearrange("c p t -> p c t")),
                      reads=[b_ST], writes=[b_VTs[cb]])
                S.dma("sp", f"vt{cb}", lambda e, vt=vt, cb=cb: e.dma_start(out=vt[:, 2:10, :], in_=STc[12:20, :, cb * 128:(cb + 1) * 128].rearrange("c p t -> p c t")),
                      reads=[b_ST], writes=[b_VTs[cb]])

                def trc(e, vt=vt):
                    r = None
                    for k in range(10):
                        r = e.transpose(out=pV[:, k, :], in_=vt[:, k, :], identity=identb[:])
                    return r
                S.op("pe", trc, reads=[b_VTs[cb], b_const], writes=[b_pV])
                S.op("act", lambda e, cb=cb: e.activation(out=Vc[:, cb, :, :], in_=pV[:], func=AF.Copy), reads=[b_pV], writes=[b_ctxkv])

            loaded = [-1]
            vtc = [0]

            def load_kv(b):
                slot = b % 8
                kr, vr = Kr[slot], Vr[slot]
                t0 = b * 128
                S.dma("sp", f"k{slot}", lambda e: e.dma_start(out=kr[:, 0:2, :], in_=ST[8:10, :, t0:t0 + 128].rearrange("c p t -> p c t")),
                      reads=[b_ST], writes=[b_Kr[slot]])
                S.dma("sp", f"k{slot}", lambda e: e.dma_start(out=kr[:, 2:10, :], in_=ST[20:28, :, t0:t0 + 128].rearrange("c p t -> p c t")),
                      reads=[b_ST], writes=[b_Kr[slot]])
                vi = vtc[0] % 2
                vtc[0] += 1
                vt = VTs[vi]
                S.dma("sp", f"vt{vi}", lambda e: e.dma_start(out=vt[:, 0:2, :], in_=ST[10:12, :, t0:t0 + 128].rearrange("c p t -> p c t")),
                      reads=[b_ST], writes=[b_VTs[vi]])
                S.dma("sp", f"vt{vi}", lambda e: e.dma_start(out=vt[:, 2:10, :], in_=ST[28:36, :, t0:t0 + 128].rearrange("c p t -> p c t")),
                      reads=[b_ST], writes=[b_VTs[vi]])

                def trv(e):
                    r = None
                    for k in range(10):
                        r = e.transpose(out=pV[:, k, :], in_=vt[:, k, :], identity=identb[:])
                    return r
                S.op("pe", trv, reads=[b_VTs[vi], b_const], writes=[b_pV])
                S.op("act", lambda e: e.activation(out=vr[:], in_=pV[:], func=AF.Copy), reads=[b_pV], writes=[b_Vr[slot]])

            def load_q(j, qi):
                qt = Qt[qi]
                t0 = j * 128
                S.dma("sp", f"q{qi}", lambda e: e.dma_start(out=qt[:, 0:8, :], in_=ST[0:8, :, t0:t0 + 128].rearrange("c p t -> p c t")),
                      reads=[b_ST], writes=[b_Qt[qi]])
                S.dma("sp", f"q{qi}", lambda e: e.dma_start(out=qt[:, 8:16, :], in_=ST[12:20, :, t0:t0 + 128].rearrange("c p t -> p c t")),
                      reads=[b_ST], writes=[b_Qt[qi]])

            ucnt = [0]
            pcn = [0]

            def attn_unit(q_ap, NQ, chunks, nmask, mask_ap, mask_bufs, sink_ap, out_ap, b_q, b_out):
                u = ucnt[0] % 2
                ucnt[0] += 1
                P = Pb[u]
                bP = b_Pb[u]
                od = pOD[u]
                bod = b_pOD[u]
                cap = 1024 // NQ
                nch = len(chunks)
                pieces = [list(range(i, min(i + cap, nch))) for i in range(0, nch, cap)]
                for piece in pieces:
                    si = pcn[0] % 2
                    pcn[0] += 1
                    ps, bps = pS[si], b_pS[si]
                    s32, bs32 = S32[si], b_S32[si]

                    def smm(e, piece=piece, ps=ps):
                        r = None
                        for li, ci in enumerate(piece):
                            r = e.matmul(ps[:, li * NQ:(li + 1) * NQ], lhsT=chunks[ci][0], rhs=q_ap, start=True, stop=True)
                        return r
                    S.op("pe", smm, reads=[b_q] + [bb for ci in piece for bb in chunks[ci][2]], writes=[bps])
                    nm = len([ci for ci in piece if ci < nmask])
                    c0 = piece[0]
                    n2 = len(piece)
                    if nm > 0:
                        S.op("act", lambda e, ps=ps, s32=s32, nm=nm: e.activation(out=s32[:, 0:nm * NQ], in_=ps[:, 0:nm * NQ], func=AF.Exp, scale=SCALE),
                             reads=[bps], writes=[bs32])
                        S.op("dve", lambda e, s32=s32, nm=nm, c0=c0: e.tensor_tensor(out=P[:, c0 * NQ:(c0 + nm) * NQ], in0=s32[:, 0:nm * NQ],
                                                                                     in1=mask_ap[:, c0 * NQ:(c0 + nm) * NQ], op=ALU.mult),
                             reads=[bs32] + mask_bufs, writes=[bP])
                    if nm < n2:
                        S.op("act", lambda e, ps=ps, nm=nm, n2=n2, c0=c0: e.activation(out=P[:, (c0 + nm) * NQ:(c0 + n2) * NQ], in_=ps[:, nm * NQ:n2 * NQ],
                                                                                       func=AF.Exp, scale=SCALE), reads=[bps], writes=[bP])
                vbs = [bb for ci in range(nch) for bb in chunks[ci][3]]
                if NQ == 128:
                    o_ps, d_ps, bo, bd = od[:, 0:128], od[:, 128:256], bod, bod
                else:
                    o_ps, d_ps, bo, bd = od[:, 0:512], pOD[1 - u][:, 0:512], bod, b_pOD[1 - u]

                def pv(e):
                    r = None
                    for ci in range(nch):
                        r = e.matmul(o_ps, lhsT=chunks[ci][1], rhs=P[:, ci * NQ:(ci + 1) * NQ], start=(ci == 0), stop=(ci == nch - 1))
                    return r

                def dn(e):
                    r = None
                    for ci in range(nch):
                        r = e.matmul(d_ps, lhsT=onesb[:], rhs=P[:, ci * NQ:(ci + 1) * NQ], start=(ci == 0), stop=(ci == nch - 1))
                    return r
                S.op("pe", pv, reads=[bP] + vbs, writes=[bo])
                S.op("pe", dn, reads=[bP, b_const], writes=[bd])
                rc, brc = rec[u], b_rec[u]
                if sink_ap is not None:
                    S.op("dve", lambda e: e.tensor_tensor(out=rc[:, 0:NQ], in0=d_ps, in1=sink_ap, op=ALU.add), reads=[bd, b_sink], writes=[brc])
                    S.op("dve", lambda e: e.reciprocal(out=rc[:, 0:NQ], in_=rc[:, 0:NQ]), reads=[brc], writes=[brc])
                else:
                    S.op("dve", lambda e: e.reciprocal(out=rc[:, 0:NQ], in_=d_ps), reads=[bd], writes=[brc])
                S.op("dve", lambda e: e.tensor_tensor(out=out_ap, in0=o_ps, in1=rc[:, 0:NQ], op=ALU.mult), reads=[bo, bd, brc], writes=[b_out])

            special = {3: 0, 4: 1, NL - 5: 2, NL - 4: 3}
            mixer_blocks = list(range(2, NL - 2))
            load_q(mixer_blocks[0], 0)
            ecnt = 0
            for idx, j in enumerate(mixer_blocks):
                qi = idx % 2
                while loaded[0] < min(j + 3, NL - 1):
                    loaded[0] += 1
                    load_kv(loaded[0])
                if idx + 1 < len(mixer_blocks):
                    load_q(mixer_blocks[idx + 1], 1 - qi)
                qt = Qt[qi]
                ot = OTb[qi]
                bot = b_OTb[qi]
                mpair = 1 if j == 3 else (2 if j == NL - 4 else 0)
                for g in range(2):
                    chunks = []
                    for bb in (j - 1, j + 1, j):
                        chunks.append((Kr[bb % 8][:, g, :], Vr[bb % 8][:, g, :], [b_Kr[bb % 8]], [b_Vr[bb % 8]]))
                    for cb in range(2):
                        chunks.append((Kc[:, g, cb * 128:(cb + 1) * 128], Vc[:, cb, g, :], [b_ctxkv], [b_ctxkv]))
                    attn_unit(qt[:, g * 4:(g + 1) * 4, :].rearrange("p a b -> p (a b)"), 512, chunks, 2, MAp[:, mpair, :], [b_tab],
                              sinkE[:, g * 4:(g + 1) * 4, :].rearrange("p a b -> p (a b)"),
                              ot[:, g * 4:(g + 1) * 4, :].rearrange("p a b -> p (a b)"), b_Qt[qi], bot)
                offs = list(range(-3, 4)) if j in special else list(range(-2, 3))
                for h in range(8):
                    if j in special:
                        et, bet = Et[ecnt % 2], b_Et[ecnt % 2]
                        ecnt += 1
                        S.op("dve", lambda e, h=h, et=et, si=special[j]: e.tensor_tensor(out=et[:], in0=expG[:, h, :, :], in1=Msp[:, si, :, :], op=ALU.mult),
                             reads=[b_tab], writes=[bet])
                        mask_ap, mbufs = et[:].rearrange("p a b -> p (a b)"), [bet]
                    else:
                        mask_ap, mbufs = Egen[:, h, :, :].rearrange("p a b -> p (a b)"), [b_Egen]
                    chunks = []
                    for o in offs:
                        bb = j + o
                        chunks.append((Kr[bb % 8][:, 2 + h, :], Vr[bb % 8][:, 2 + h, :], [b_Kr[bb % 8]], [b_Vr[bb % 8]]))
                    for cb in range(2):
                        chunks.append((Kc[:, 2 + h, cb * 128:(cb + 1) * 128], Vc[:, cb, 2 + h, :], [b_ctxkv], [b_ctxkv]))
                    attn_unit(qt[:, 8 + h, :], 128, chunks, len(offs), mask_ap, mbufs, None, ot[:, 8 + h, :], b_Qt[qi], bot)
                m0 = (j - 2) * 128
                S.dma("sp", f"ot{qi}", lambda e, ot=ot, m0=m0: e.dma_start(out=OTs[:, :, m0:m0 + 128].rearrange("c p t -> p c t"), in_=ot[:]),
                      reads=[bot], writes=[b_OTs])
            S.barrier()
            S.emit()

        b_ZT = Buf()
        b_X1 = Buf()
        mgroups = [(o * 128, n * 128) for (o, n) in split_even(NM, 4)]
        with ExitStack() as ph:
            wbr = sbt(ph, "wbr", [128, 16, D], BF16)
            OTl = [sbt(ph, f"OTl{i}", [128, 16, 512], BF16) for i in range(2)]
            gl = [sbt(ph, f"gl{i}", [128, 2, 512], BF16) for i in range(3)]
            ta = [sbt(ph, f"ta{i}", [128, 512], F32) for i in range(2)]
            tb = [sbt(ph, f"tb{i}", [128, 512], F32) for i in range(2)]
            zst = [sbt(ph, f"zst{i}", [128, 16, 512], BF16) for i in range(2)]
            pA = [pst(ph, f"pA{i}", [128, 512], F32) for i in range(2)]
            pB = [pst(ph, f"pB{i}", [128, 512], F32) for i in range(2)]
            b_wbr = Buf()
            b_OTl = [Buf() for _ in range(2)]
            b_gl = [Buf() for _ in range(3)]
            b_ta = [Buf() for _ in range(2)]
            b_tb = [Buf() for _ in range(2)]
            b_zst = [Buf() for _ in range(2)]
            b_pA = [Buf() for _ in range(2)]
            b_pB = [Buf() for _ in range(2)]
            for i in range(4):
                S.dma("pool", f"w{i % 3}", lambda e, i=i: e.dma_start(out=wbr[:, 0:8, i * 512:(i + 1) * 512],
                                                                      in_=w_br_a[:, i * 512:(i + 1) * 512].rearrange("(k p) n -> p k n", p=128)), writes=[b_wbr])
                S.dma("pool", f"w{(i + 1) % 3}", lambda e, i=i: e.dma_start(out=wbr[:, 8:16, i * 512:(i + 1) * 512],
                                                                            in_=w_br_b[:, i * 512:(i + 1) * 512].rearrange("(k p) n -> p k n", p=128)), writes=[b_wbr])
            gcnt = 0
            fc = 0
            for gi, (m0, gn) in enumerate(mgroups):
                oi = gi % 2
                otl = OTl[oi]
                S.dma("sp", f"otl{oi}", lambda e, otl=otl, m0=m0, gn=gn: e.dma_start(out=otl[:, :, 0:gn], in_=OTs[:, :, m0:m0 + gn].rearrange("c p t -> p c t")),
                      reads=[b_OTs], writes=[b_OTl[oi]])
                zs = zst[oi]
                for f in range(16):
                    g_ = gl[gcnt % 3]
                    bg = b_gl[gcnt % 3]
                    gname = f"gl{gcnt % 3}"
                    gcnt += 1
                    l0 = m0 + 256
                    S.dma("sp", gname, lambda e, g_=g_, f=f, l0=l0, gn=gn: e.dma_start(out=g_[:, 0, 0:gn], in_=ST[36 + f, :, l0:l0 + gn]), reads=[b_ST], writes=[bg])
                    S.dma("sp", gname, lambda e, g_=g_, f=f, l0=l0, gn=gn: e.dma_start(out=g_[:, 1, 0:gn], in_=ST[52 + f, :, l0:l0 + gn]), reads=[b_ST], writes=[bg])
                    pa, pb_, bpa, bpb = pA[fc % 2], pB[fc % 2], b_pA[fc % 2], b_pB[fc % 2]
                    ta_, tb_, bta, btb = ta[fc % 2], tb[fc % 2], b_ta[fc % 2], b_tb[fc % 2]
                    fc += 1

                    def mmA(e, pa=pa, f=f, otl=otl, gn=gn):
                        r = None
                        for k in range(8):
                            r = e.matmul(pa[:, 0:gn], lhsT=wbr[:, k, f * 128:(f + 1) * 128], rhs=otl[:, k, 0:gn], start=(k == 0), stop=(k == 7))
                        return r

                    def mmB(e, pb_=pb_, f=f, otl=otl, gn=gn):
                        r = None
                        for k in range(8):
                            r = e.matmul(pb_[:, 0:gn], lhsT=wbr[:, 8 + k, f * 128:(f + 1) * 128], rhs=otl[:, 8 + k, 0:gn], start=(k == 0), stop=(k == 7))
                        return r
                    S.op("pe", mmA, reads=[b_wbr, b_OTl[oi]], writes=[bpa])
                    S.op("pe", mmB, reads=[b_wbr, b_OTl[oi]], writes=[bpb])
                    S.op("dve", lambda e, ta_=ta_, pa=pa, g_=g_, gn=gn: e.tensor_tensor(out=ta_[:, 0:gn], in0=pa[:, 0:gn], in1=g_[:, 0, 0:gn], op=ALU.mult),
                         reads=[bpa, bg], writes=[bta])
                    S.op("dve", lambda e, tb_=tb_, pb_=pb_, g_=g_, gn=gn: e.tensor_tensor(out=tb_[:, 0:gn], in0=pb_[:, 0:gn], in1=g_[:, 1, 0:gn], op=ALU.mult),
                         reads=[bpb, bg], writes=[btb])
                    S.op("dve", lambda e, zs=zs, f=f, ta_=ta_, tb_=tb_, gn=gn: e.tensor_tensor(out=zs[:, f, 0:gn], in0=ta_[:, 0:gn], in1=tb_[:, 0:gn], op=ALU.add),
                         reads=[bta, btb], writes=[b_zst[oi]])
                S.dma("sp", f"zs{oi}", lambda e, zs=zs, m0=m0, gn=gn: e.dma_start(out=ZT[:, :, m0:m0 + gn].rearrange("c p t -> p c t"), in_=zs[:, :, 0:gn]),
                      reads=[b_zst[oi]], writes=[b_ZT])
            S.barrier()
            S.emit()

        with ExitStack() as ph:
            wo = sbt(ph, "wo", [128, 16, D], BF16)
            zl = [sbt(ph, f"zl{i}", [128, 16, 512], BF16) for i in range(2)]
            epi = Epi(ph, "m")
            pY = [pst(ph, f"pY{i}", [128, 1024], F32) for i in range(4)]
            b_wo = Buf()
            b_zl = [Buf() for _ in range(2)]
            b_pY = [Buf() for _ in range(4)]
            for i in range(4):
                S.dma("pool", f"w{i % 3}", lambda e, i=i: e.dma_start(out=wo[:, :, i * 512:(i + 1) * 512],
                                                                      in_=w_o[:, i * 512:(i + 1) * 512].rearrange("(k p) n -> p k n", p=128)), writes=[b_wo])
            bc = 0
            for gi, (m0, gn) in enumerate(mgroups):
                zi = gi % 2
                z_ = zl[zi]
                S.dma("sp", f"zl{zi}", lambda e, z_=z_, m0=m0, gn=gn: e.dma_start(out=z_[:, :, 0:gn], in_=ZT[:, :, m0:m0 + gn].rearrange("c p t -> p c t")),
                      reads=[b_ZT], writes=[b_zl[zi]])
                for bi in range(gn // 128):
                    mb = m0 // 128 + bi
                    lj = mb + 2
                    pys = [pY[(bc % 2) * 2], pY[(bc % 2) * 2 + 1]]
                    bpys = [b_pY[(bc % 2) * 2], b_pY[(bc % 2) * 2 + 1]]
                    bc += 1
                    epi.load_res(x_loc[lj * 128:(lj + 1) * 128, :])
                    for hh in range(2):
                        def mmo(e, hh=hh, z_=z_, bi=bi, py=pys[hh]):
                            r = None
                            for n in range(2):
                                for k in range(16):
                                    r = e.matmul(py[:, n * 512:(n + 1) * 512], lhsT=z_[:, k, bi * 128:(bi + 1) * 128],
                                                 rhs=wo[:, k, (hh * 2 + n) * 512:(hh * 2 + n + 1) * 512], start=(k == 0), stop=(k == 15))
                            return r
                        S.op("pe", mmo, reads=[b_wo, b_zl[zi]], writes=[bpys[hh]])
                    epi.run([pys[0][:], pys[1][:]], bpys, G1, X1[mb * 128:(mb + 1) * 128, :])
            S.barrier()
            S.emit()

        with ExitStack() as ph:
            T = 512
            h2T = sbt(ph, "h2T", [128, 16, T + 2], BF16)
            actT = sbt(ph, "actT", [128, NF, T], BF16)
            y2T = sbt(ph, "y2T", [128, 16, T], F32)
            cw = sbt(ph, "cw", [128, 88, 3], F32)
            cb_ = sbt(ph, "cb", [128, 88], F32)
            edge = sbt(ph, "edge", [128, 2], F32)
            wu = [sbt(ph, f"wu{i}", [128, 16, 256], BF16) for i in range(3)]
            wd = [sbt(ph, f"wd{i}", [128, NF, 128], BF16) for i in range(2)]
            tA = [sbt(ph, f"tA{i}", [128, 512], F32) for i in range(2)]
            tG = [sbt(ph, f"tG{i}", [128, 512], F32) for i in range(2)]
            ln = LN(ph, "f", nbuf=1)
            epi = Epi(ph, "f", nbuf=1)
            pU = [pst(ph, f"pU{i}", [128, 1024], F32) for i in range(2)]
            pZ = [pst(ph, f"pZ{i}", [128, 512], F32) for i in range(2)]
            b_h2T, b_actT, b_y2T, b_cw = Buf(), Buf(), Buf(), Buf()
            b_wu = [Buf() for _ in range(3)]
            b_wd = [Buf() for _ in range(2)]
            b_tA = [Buf() for _ in range(2)]
            b_tG = [Buf() for _ in range(2)]
            b_pU = [Buf() for _ in range(2)]
            b_pZ = [Buf() for _ in range(2)]
            S.dma("sp", "misc", lambda e: e.dma_start(out=cw[:], in_=cwT_in), writes=[b_cw])
            S.dma("sp", "misc", lambda e: e.dma_start(out=cb_[:], in_=cbT_in), writes=[b_cw])
            S.dma("sp", "misc", lambda e: e.dma_start(out=edge[:], in_=edge_in), writes=[b_cw])
            ntile = NOT // T
            wuc = 0
            wdc = 0
            tc_ = 0
            for ti in range(ntile):
                s = 128 + ti * T
                r0 = s - 1
                nrow_tiles = [(i * 128, 128) for i in range(T // 128)] + [(T, 2)]
                ln.load(X1[r0:r0 + 128, :], 128)
                for ri, (ro, rn) in enumerate(nrow_tiles):
                    ln.run(rn, vec[:, 4, :], vec[:, 5, :], lambda k, ro=ro, rn=rn: h2T[:, k, ro:ro + rn], b_h2T)
                    if ri + 1 < len(nrow_tiles):
                        ro2, rn2 = nrow_tiles[ri + 1]
                        ln.load(X1[r0 + ro2:r0 + ro2 + rn2, :], rn2)
                if ti == 0:
                    S.op("dve", lambda e: e.tensor_scalar(out=h2T[:, :, 0:1], in0=h2T[:, :, 0:1], scalar1=edge[:, 0:1], scalar2=None, op0=ALU.mult),
                         reads=[b_cw], writes=[b_h2T])
                if ti == ntile - 1:
                    S.op("dve", lambda e: e.tensor_scalar(out=h2T[:, :, T + 1:T + 2], in0=h2T[:, :, T + 1:T + 2], scalar1=edge[:, 1:2], scalar2=None, op0=ALU.mult),
                         reads=[b_cw], writes=[b_h2T])
                for f in range(NF):
                    w_ = wu[wuc % 3]
                    bw = b_wu[wuc % 3]
                    wname = f"w{wuc % 3}"
                    wuc += 1
                    S.dma("pool", wname, lambda e, w_=w_, f=f: e.dma_start(out=w_[:, :, 0:128], in_=w_up[:, f * 128:(f + 1) * 128].rearrange("(k p) n -> p k n", p=128)),
                          writes=[bw])
                    S.dma("pool", wname, lambda e, w_=w_, f=f: e.dma_start(out=w_[:, :, 128:256],
                                                                           in_=w_up[:, DFF + f * 128:DFF + (f + 1) * 128].rearrange("(k p) n -> p k n", p=128)),
                          writes=[bw])
                    for part in range(2):
                        pu, bpu = pU[part], b_pU[part]

                        def mmu(e, pu=pu, w_=w_, part=part):
                            r = None
                            for k in range(16):
                                r = e.matmul(pu[:, 254:512], lhsT=w_[:, k, part * 128:(part + 1) * 128], rhs=h2T[:, k, 0:258], start=(k == 0), stop=(k == 15))
                            for k in range(16):
                                r = e.matmul(pu[:, 512:770], lhsT=w_[:, k, part * 128:(part + 1) * 128], rhs=h2T[:, k, 256:514], start=(k == 0), stop=(k == 15))
                            return r
                        S.op("pe", mmu, reads=[bw, b_h2T], writes=[bpu])
                        fc_ = part * NF + f
                        tt = (tA if part == 0 else tG)[tc_ % 2]
                        btt = (b_tA if part == 0 else b_tG)[tc_ % 2]
                        U = [pu[:, 254 + sft:254 + sft + 516].rearrange("p (h i) -> p h i", h=2)[:, :, 0:256] for sft in range(3)]
                        tv = tt[:].rearrange("p (h i) -> p h i", h=2)
                        S.op("act", lambda e, U=U, tv=tv, fc_=fc_: e.activation(out=tv, in_=U[0], func=AF.Identity, scale=cw[:, fc_, 0:1], bias=cb_[:, fc_:fc_ + 1]),
                             reads=[bpu, b_cw], writes=[btt])
                        S.op("dve", lambda e, U=U, tv=tv, fc_=fc_: e.scalar_tensor_tensor(out=tv, in0=U[1], scalar=cw[:, fc_, 1:2], in1=tv, op0=ALU.mult, op1=ALU.add),
                             reads=[bpu, b_cw], writes=[btt])
                        S.op("dve", lambda e, U=U, tv=tv, fc_=fc_: e.scalar_tensor_tensor(out=tv, in0=U[2], scalar=cw[:, fc_, 2:3], in1=tv, op0=ALU.mult, op1=ALU.add),
                             reads=[bpu, b_cw], writes=[btt])
                    ta_, tg_ = tA[tc_ % 2], tG[tc_ % 2]
                    bta, btg = b_tA[tc_ % 2], b_tG[tc_ % 2]
                    tc_ += 1
                    S.op("act", lambda e, tg_=tg_: e.activation(out=tg_[:], in_=tg_[:], func=AF.Silu), reads=[btg], writes=[btg])
                    S.op("dve", lambda e, ta_=ta_, tg_=tg_, f=f: e.tensor_tensor(out=actT[:, f, :], in0=ta_[:], in1=tg_[:], op=ALU.mult),
                         reads=[bta, btg], writes=[b_actT])
                for o in range(16):
                    w_ = wd[wdc % 2]
                    bw = b_wd[wdc % 2]
                    wname = f"wd{wdc % 2}"
                    wdc += 1
                    S.dma("pool", wname, lambda e, w_=w_, o=o: e.dma_start(out=w_[:], in_=w_down[:, o * 128:(o + 1) * 128].rearrange("(k p) n -> p k n", p=128)),
                          writes=[bw])
                    pz, bpz = pZ[o % 2][:], b_pZ[o % 2]

                    def mmd(e, pz=pz, w_=w_):
                        r = None
                        for k in range(NF):
                            r = e.matmul(pz, lhsT=w_[:, k, :], rhs=actT[:, k, :], start=(k == 0), stop=(k == NF - 1))
                        return r
                    S.op("pe", mmd, reads=[bw, b_actT], writes=[bpz])
                    S.op("act", lambda e, pz=pz, o=o: e.activation(out=y2T[:, o, :], in_=pz, func=AF.Copy), reads=[bpz], writes=[b_y2T])
                for bi in range(T // 128):
                    epi.load_res(X1[s + bi * 128:s + (bi + 1) * 128, :])
                    halves = [pU[0], pU[1]]
                    bh = [b_pU[0], b_pU[1]]
                    for hh in range(2):
                        def trb(e, hh=hh, bi=bi, ph_=halves[hh]):
                            r = None
                            for o in range(8):
                                r = e.matmul(ph_[:, o * 128:(o + 1) * 128], lhsT=y2T[:, hh * 8 + o, bi * 128:(bi + 1) * 128], rhs=ident[:], start=True, stop=True)
                            return r
                        S.op("pe", trb, reads=[b_y2T, b_const], writes=[bh[hh]])
                    oo = ti * T + bi * 128
                    epi.run([halves[0][:], halves[1][:]], bh, G2, out[oo:oo + 128, :])
            S.barrier()
            S.emit()
    return nc


def _fm(v, k=16):
    return np.ascontiguousarray(np.asarray(v, np.float32).reshape(k, 128).T)


def _geometry_tables(NOWN, half):
    NL = NOWN + 6
    NBS = 2 * NOWN
    rows = NBS * 2
    SEQ = NBS * 128
    kk = np.arange(128)
    kr, ck = kk // 64, kk % 64
    qr, cq = kk // 64, kk % 64
    col_start = np.clip(cq - NB_COLS // 2, 0, GRID_W - NB_COLS)
    colvalid = (ck[:, None] >= col_start[None, :]) & (ck[:, None] < col_start[None, :] + NB_COLS)

    def valid(g, o):
        gb = g + o
        if gb < 0 or gb >= NBS or g < 0 or g >= NBS:
            return np.zeros((128, 128), np.float32)
        krow = 2 * gb + kr
        r = 2 * g + qr
        start = np.clip(r - NB_ROWS // 2, 0, rows - NB_ROWS)
        rowvalid = (krow[:, None] >= start[None, :]) & (krow[:, None] < start[None, :] + NB_ROWS)
        return (rowvalid & colvalid).astype(np.float32)

    g0 = NBS // 2
    Mgen = np.stack([valid(g0, o) for o in range(-2, 3)], axis=1)
    spj = [3, 4, NL - 5, NL - 4]
    Msp = np.stack([np.stack([valid(half * NOWN - 3 + j, o) for o in range(-3, 4)], axis=1) for j in spj], axis=1)
    triP = (kk[:, None] >= kk[None, :]).astype(np.float32)
    triN = (kk[:, None] <= kk[None, :]).astype(np.float32)
    z = np.zeros_like(triP)
    g3 = half * NOWN - 3 + 3
    gl = half * NOWN - 3 + (NL - 4)
    pairs = [(triP, triN), (triP if g3 - 1 >= 0 else z, triN), (triP, triN if gl + 1 < NBS else z)]
    MA = np.stack([np.concatenate([np.tile(a, (1, 4)), np.tile(b, (1, 4))], axis=1) for a, b in pairs], axis=1)
    dr = np.stack([np.clip(2 * o + kr[:, None] - qr[None, :] + (NB_ROWS - 1), 0, 2 * NB_ROWS - 2) for o in range(-3, 4)], axis=0)
    dc = np.clip(ck[:, None] - cq[None, :], -(NB_COLS - 1), NB_COLS - 1) + (NB_COLS - 1)
    t = (half * NOWN - 3) * 128 + np.arange(NL * 128)
    t = np.clip(t, 0, SEQ - 1)
    row, col = (t // GRID_W).astype(np.float32), (t % GRID_W).astype(np.float32)
    inv = (ROPE_BASE ** (-np.arange(32, dtype=np.float32) / 32)).astype(np.float32)
    p = np.arange(128)
    pos = np.where((p < 64)[:, None], row[None, :], col[None, :]).astype(np.float32)
    ang = (pos * inv[p % 32][:, None]).astype(np.float32)
    cosT = np.cos(ang).astype(np.float32)
    sgn = np.where((p % 64) < 32, -1.0, 1.0).astype(np.float32)[:, None]
    sinT = (np.sin(ang).astype(np.float32) * sgn).astype(np.float32)
    partner = np.where((p % 64) < 32, p + 32, p - 32)
    perm = np.zeros((128, 128), np.float32)
    perm[partner, p] = 1.0
    edge = np.tile(np.array([[1.0 if half == 1 else 0.0, 1.0 if half == 0 else 0.0]], np.float32), (128, 1))
    return dict(Mgen=Mgen, Msp=Msp, MA=MA, dr=dr, dc=dc, ropec=cosT, ropes=sinT, perm=perm, edge=edge)


_NC_CACHE = {}


def kernel(x, c, ctx, c_ctx, w_mod, b_mod, g_attn_pre, g_attn_post, g_ffn_pre, g_ffn_post,
           w_in, sink_a, rpb_b, w_br_a, w_br_b, w_o, w_up, conv_w, conv_b, w_down):
    x = np.asarray(x, np.float32)
    B, SEQ, _ = x.shape
    NBS = SEQ // 128
    NOWN = NBS // 2
    NL = NOWN + 6
    f32 = lambda a: np.ascontiguousarray(np.asarray(a, np.float32))
    w_mod0, w_in0, w_br_a0, w_br_b0, w_o0, w_up0, w_down0 = [f32(w[0]) for w in (w_mod, w_in, w_br_a, w_br_b, w_o, w_up, w_down)]
    bmT = _fm(np.asarray(b_mod)[0], 96)
    gv = np.ascontiguousarray(np.stack([_fm(np.asarray(g)[0]) for g in (g_attn_pre, g_attn_post, g_ffn_pre, g_ffn_post)], axis=1))
    cw = np.asarray(conv_w, np.float32)[0]
    cwT = np.ascontiguousarray(cw.reshape(3, 88, 128).transpose(2, 1, 0))
    cbT = _fm(np.asarray(conv_b)[0], 88)
    sinkb = np.ascontiguousarray(np.broadcast_to(np.asarray(sink_a, np.float32)[0][None, :, None], (128, 8, 128)))
    rpb = np.asarray(rpb_b, np.float32)[0]
    ident = np.eye(128, dtype=np.float32)
    geo = [_geometry_tables(NOWN, h) for h in range(2)]
    Gt = rpb[:, geo[0]["dr"], geo[0]["dc"][None]]
    Gtab = np.ascontiguousarray(Gt.transpose(2, 0, 1, 3))
    in_maps = []
    for core in range(8):
        b, half = core // 2, core % 2
        g = geo[half]
        xl = np.zeros((NL * 128, D), np.float32)
        lo = (half * NOWN - 3) * 128
        s0, s1 = max(lo, 0), min(lo + NL * 128, SEQ)
        xl[s0 - lo:s1 - lo] = x[b, s0:s1]
        cT = np.ascontiguousarray(np.stack([_fm(np.asarray(c)[b]), _fm(np.asarray(c_ctx))], axis=2))
        in_maps.append(dict(
            x_loc=xl, ctx=f32(np.asarray(ctx)[b]), cT=cT, bmT=bmT, gv=gv, w_mod=w_mod0, w_in=w_in0, w_br_a=w_br_a0, w_br_b=w_br_b0,
            w_o=w_o0, w_up=w_up0, w_down=w_down0, cwT=cwT, cbT=cbT, sinkb=sinkb, Gtab=Gtab, Mgen=f32(g["Mgen"]), Msp=f32(g["Msp"]),
            MA=f32(g["MA"]), ropec=f32(g["ropec"]), ropes=f32(g["ropes"]), perm=g["perm"], ident=ident, edge=f32(g["edge"])))
    if NOWN not in _NC_CACHE:
        _NC_CACHE[NOWN] = build(NOWN)
    nc = _NC_CACHE[NOWN]
    res = run_bass_kernel_spmd(nc, in_maps, core_ids=list(range(8)))
    out = np.zeros((B, SEQ, D), np.float32)
    for core in range(8):
        b, half = core // 2, core % 2
        out[b, half * NOWN * 128:(half + 1) * NOWN * 128] = res.results[core]["out"]
    return out
```

```python
import numpy as np
from contextlib import ExitStack
import concourse.bass as bass
import concourse.mybir as mybir
from concourse.bass_utils import run_bass_kernel_spmd

F32 = mybir.dt.float32
BF16 = mybir.dt.bfloat16
AF = mybir.ActivationFunctionType
ALU = mybir.AluOpType

D = 2048
DFF = 5632
NF = DFF // 128
HD = 128
GRID_W = 64
CTX = 256
EPS = 1e-6
ROPE_BASE = 10000.0
NB_ROWS, NB_COLS = 8, 16
SCALE = HD ** -0.5


class Sem:
    def __init__(self, handle, name):
        self.h = handle
        self.name = name
        self.count = 0


class Buf:
    __slots__ = ("name", "w", "r")

    def __init__(self, name=""):
        self.name = name
        self.w = None
        self.r = []


class Sched:
    ENGS = ("pe", "act", "dve", "pool", "sp")

    def __init__(self, nc, stack):
        self.nc = nc
        self.stack = stack
        self.prog = {e: [] for e in self.ENGS}
        self.esem = {e: self.new_sem("e_" + e) for e in self.ENGS}
        self.seen = {e: {} for e in self.ENGS}
        self.dsems = {}

    def new_sem(self, name):
        h = self.stack.enter_context(self.nc.semaphore(name))
        return Sem(h, name)

    def dsem(self, name):
        if name not in self.dsems:
            self.dsems[name] = self.new_sem("d_" + name)
        return self.dsems[name]

    def _deps(self, reads, writes):
        deps = []
        for b in reads:
            if b.w is not None:
                deps.append(b.w)
        for b in writes:
            if b.w is not None:
                deps.append(b.w)
            deps.extend(b.r)
        return deps

    def _emit_waits(self, eng, deps):
        seen = self.seen[eng]
        need = {}
        for (s, v) in deps:
            if seen.get(s, 0) >= v:
                continue
            if need.get(s, 0) < v:
                need[s] = v
        for s, v in need.items():
            seen[s] = v
            self.prog[eng].append(lambda e, s=s, v=v: e.wait_ge(s.h, v))

    def _commit(self, tok, reads, writes):
        for b in writes:
            b.w = tok
            b.r = []
        for b in reads:
            if b in writes:
                continue
            b.r.append(tok)
            if len(b.r) > 16:
                mx = {}
                for (s, v) in b.r:
                    if mx.get(s, 0) < v:
                        mx[s] = v
                b.r = list(mx.items())

    def op(self, eng, fn, reads=(), writes=()):
        reads = list(reads)
        writes = list(writes)
        self._emit_waits(eng, self._deps(reads, writes))
        s = self.esem[eng]
        s.count += 1
        tok = (s, s.count)
        self.prog[eng].append(lambda e, fn=fn, s=s: fn(e).then_inc(s.h, 1))
        self.seen[eng][s] = max(self.seen[eng].get(s, 0), 0)
        self._commit(tok, reads, writes)
        return tok

    def dma(self, eng, semname, fn, reads=(), writes=()):
        sem = self.dsem(semname)
        reads = list(reads)
        writes = list(writes)
        deps = self._deps(reads, writes)
        if sem.count > 0:
            deps.append((sem, sem.count))
        self._emit_waits(eng, deps)
        sem.count += 16
        tok = (sem, sem.count)
        self.prog[eng].append(lambda e, fn=fn, sem=sem: fn(e).then_inc(sem.h, 16))
        self._commit(tok, reads, writes)
        return tok

    def barrier(self):
        toks = [(s, s.count) for s in self.esem.values() if s.count > 0]
        toks += [(s, s.count) for s in self.dsems.values() if s.count > 0]
        for e in self.ENGS:
            self._emit_waits(e, toks)

    def emit(self):
        nc = self.nc
        prog = self.prog
        with nc.Block() as block:
            @block.tensor
            def _(e):
                for f in prog["pe"]:
                    f(e)

            @block.scalar
            def _(e):
                for f in prog["act"]:
                    f(e)

            @block.vector
            def _(e):
                for f in prog["dve"]:
                    f(e)

            @block.gpsimd
            def _(e):
                for f in prog["pool"]:
                    f(e)

            @block.sync
            def _(e):
                for f in prog["sp"]:
                    f(e)
        self.prog = {e: [] for e in self.ENGS}


def split_even(n, mx, q=1):
    k = -(-n // mx)
    base = (n // k) // q * q
    sizes = [base] * k
    rem = n - base * k
    i = 0
    while rem > 0:
        sizes[i] += q
        rem -= q
        i += 1
    offs = np.cumsum([0] + sizes[:-1]).tolist()
    return list(zip(offs, sizes))


def build(NOWN):
    NL = NOWN + 6
    NM = NOWN + 2
    NLT, NMT, NOT = NL * 128, NM * 128, NOWN * 128
    nc = bass.Bass("TRN2", target_bir_lowering=False)

    def din(name, shape, dt=F32):
        return nc.dram_tensor(name, list(shape), dt, kind="ExternalInput").ap()

    x_loc = din("x_loc", [NLT, D])
    ctx_in = din("ctx", [CTX, D])
    cT_in = din("cT", [128, 16, 2])
    bmT_in = din("bmT", [128, 96])
    gv_in = din("gv", [128, 4, 16])
    w_mod = din("w_mod", [D, 6 * D])
    w_in = din("w_in", [D, 8704])
    w_br_a = din("w_br_a", [1024, D])
    w_br_b = din("w_br_b", [1024, D])
    w_o = din("w_o", [D, D])
    w_up = din("w_up", [D, 2 * DFF])
    w_down = din("w_down", [DFF, D])
    cwT_in = din("cwT", [128, 88, 3])
    cbT_in = din("cbT", [128, 88])
    sinkb_in = din("sinkb", [128, 8, 128])
    Gtab_in = din("Gtab", [128, 8, 7, 128])
    Mgen_in = din("Mgen", [128, 5, 128])
    Msp_in = din("Msp", [128, 4, 7, 128])
    MA_in = din("MA", [128, 3, 1024])
    ropec = din("ropec", [128, NLT])
    ropes = din("ropes", [128, NLT])
    perm_in = din("perm", [128, 128])
    ident_in = din("ident", [128, 128])
    edge_in = din("edge", [128, 2])
    out = nc.dram_tensor("out", [NOT, D], F32, kind="ExternalOutput").ap()
    ST = nc.dram_tensor("ST", [68, 128, NLT], BF16).ap()
    STc = nc.dram_tensor("STc", [20, 128, CTX], BF16).ap()
    OTs = nc.dram_tensor("OTs", [16, 128, NMT], BF16).ap()
    ZT = nc.dram_tensor("ZT", [16, 128, NMT], BF16).ap()
    X1 = nc.dram_tensor("X1", [NMT, D], F32).ap()
    WU = nc.dram_tensor("WU", [NF, 128, 16, 256], BF16).ap()
    WD = nc.dram_tensor("WD", [16, 128, NF, 128], BF16).ap()

    with ExitStack() as top:
        S = Sched(nc, top)

        def sbt(stack, name, shape, dt):
            return stack.enter_context(nc.sbuf_tensor("s_" + name, list(shape), dt))

        def pst(stack, name, shape, dt=F32):
            return stack.enter_context(nc.psum_tensor("p_" + name, list(shape), dt))

        ident = sbt(top, "ident", [128, 128], F32)
        identb = sbt(top, "identb", [128, 128], BF16)
        onesf = sbt(top, "onesf", [128, 128], F32)
        onesb = sbt(top, "onesb", [128, 128], BF16)
        epst = sbt(top, "epst", [128, 1], F32)
        vec = sbt(top, "vec", [128, 8, 16], F32)
        G1 = sbt(top, "G1", [128, D], F32)
        G2 = sbt(top, "G2", [128, D], F32)
        b_const = Buf("const")
        b_vec = Buf("vec")
        b_G = Buf("G")
        S.dma("sp", "misc", lambda e: e.dma_start(out=ident[:], in_=ident_in), writes=[b_const])
        S.op("dve", lambda e: e.tensor_copy(out=identb[:], in_=ident[:]), reads=[b_const], writes=[b_const])
        S.op("dve", lambda e: e.memset(onesf[:], 1.0), writes=[b_const])
        S.op("dve", lambda e: e.memset(onesb[:], 1.0), writes=[b_const])
        S.op("dve", lambda e: e.memset(epst[:], EPS), writes=[b_const])

        class LN:
            def __init__(self, stack, tag, nbuf=2):
                self.nbuf = nbuf
                self.xt = [sbt(stack, f"ln_xt{tag}{i}", [128, D], F32) for i in range(nbuf)]
                self.xn = [sbt(stack, f"ln_xn{tag}{i}", [128, D], BF16) for i in range(nbuf)]
                self.junk = sbt(stack, f"ln_junk{tag}", [128, D], BF16)
                self.ss = [sbt(stack, f"ln_ss{tag}{i}", [128, 4], F32) for i in range(nbuf)]
                self.pT = [pst(stack, f"ln_pT{tag}{i}", [128, 16, 128], BF16) for i in range(nbuf)]
                self.b_xt = [Buf() for _ in range(nbuf)]
                self.b_xn = [Buf() for _ in range(nbuf)]
                self.b_ss = [Buf() for _ in range(nbuf)]
                self.b_pT = [Buf() for _ in range(nbuf)]
                self.b_junk = Buf()
                self.i = 0
                self.tag = tag

            def load(self, src, rows):
                i = self.i % self.nbuf
                xt = self.xt[i]
                S.dma("sp", f"lnx{self.tag}{i}", lambda e: e.dma_start(out=xt[0:rows, :], in_=src), writes=[self.b_xt[i]])

            def run(self, rows, sc_ap, sh_ap, dst_fn, b_dst):
                i = self.i % self.nbuf
                self.i += 1
                xt, xn, ss, pT, junk = self.xt[i], self.xn[i], self.ss[i], self.pT[i], self.junk
                bx, bn, bs, bp = self.b_xt[i], self.b_xn[i], self.b_ss[i], self.b_pT[i]
                S.op("dve", lambda e: e.memset(ss[:], 0.0), writes=[bs])
                S.op("act", lambda e: e.activation(out=junk[0:rows, :], in_=xt[0:rows, :], func=AF.Square,
                                                   accum_out=ss[0:rows, 0:1]), reads=[bx], writes=[self.b_junk, bs])
                S.op("act", lambda e: e.activation(out=ss[0:rows, 1:2], in_=ss[0:rows, 0:1], func=AF.Sqrt,
                                                   scale=1.0 / D, bias=epst[0:rows, 0:1]), reads=[bs, b_const], writes=[bs])
                S.op("dve", lambda e: e.reciprocal(out=ss[0:rows, 2:3], in_=ss[0:rows, 1:2]), reads=[bs], writes=[bs])
                S.op("dve", lambda e: e.tensor_scalar(out=xn[0:rows, :], in0=xt[0:rows, :], scalar1=ss[0:rows, 2:3],
                                                      scalar2=None, op0=ALU.mult), reads=[bx, bs], writes=[bn])

                def tr(e):
                    r = None
                    for k in range(16):
                        r = e.transpose(out=pT[:, k, 0:rows], in_=xn[0:rows, k * 128:(k + 1) * 128],
                                        identity=identb[0:rows, 0:rows])
                    return r
                S.op("pe", tr, reads=[bn, b_const], writes=[bp])

                def ev(e):
                    r = None
                    for k in range(16):
                        r = e.activation(out=dst_fn(k), in_=pT[:, k, 0:rows], func=AF.Identity,
                                         scale=sc_ap[:, k:k + 1], bias=sh_ap[:, k:k + 1])
                    return r
                S.op("act", ev, reads=[bp, b_vec], writes=[b_dst])

        class Epi:
            def __init__(self, stack, tag, nbuf=2):
                self.nbuf = nbuf
                self.junk = sbt(stack, f"ep_junk{tag}", [128, 1024], BF16)
                self.ss = [sbt(stack, f"ep_ss{tag}{i}", [128, 8], F32) for i in range(nbuf)]
                self.xr = [sbt(stack, f"ep_xr{tag}{i}", [128, D], F32) for i in range(nbuf)]
                self.xo = [sbt(stack, f"ep_xo{tag}{i}", [128, D], F32) for i in range(nbuf)]
                self.b_ss = [Buf() for _ in range(nbuf)]
                self.b_junk = Buf()
                self.b_xr = [Buf() for _ in range(nbuf)]
                self.b_xo = [Buf() for _ in range(nbuf)]
                self.i = 0
                self.tag = tag

            def load_res(self, src):
                i = self.i % self.nbuf
                xr = self.xr[i]
                S.dma("sp", f"epr{self.tag}{i}", lambda e: e.dma_start(out=xr[:], in_=src), writes=[self.b_xr[i]])

            def run(self, pYs, b_pYs, Gt, dst):
                i = self.i % self.nbuf
                self.i += 1
                ss, xr, xo, junk = self.ss[i], self.xr[i], self.xo[i], self.junk
                bs = self.b_ss[i]
                S.op("dve", lambda e: e.memset(ss[:], 0.0), writes=[bs])
                for h in range(2):
                    S.op("act", lambda e, h=h: e.activation(out=junk[:], in_=pYs[h], func=AF.Square,
                                                            accum_out=ss[:, h:h + 1]), reads=[b_pYs[h]], writes=[self.b_junk, bs])
                S.op("dve", lambda e: e.tensor_tensor(out=ss[:, 2:3], in0=ss[:, 0:1], in1=ss[:, 1:2], op=ALU.add), reads=[bs], writes=[bs])
                S.op("act", lambda e: e.activation(out=ss[:, 3:4], in_=ss[:, 2:3], func=AF.Sqrt, scale=1.0 / D, bias=epst[:, 0:1]),
                     reads=[bs, b_const], writes=[bs])
                S.op("dve", lambda e: e.reciprocal(out=ss[:, 4:5], in_=ss[:, 3:4]), reads=[bs], writes=[bs])
                for h in range(2):
                    S.op("dve", lambda e, h=h: e.scalar_tensor_tensor(out=xo[:, h * 1024:(h + 1) * 1024], in0=pYs[h], scalar=ss[:, 4:5],
                                                                      in1=Gt[:, h * 1024:(h + 1) * 1024], op0=ALU.mult, op1=ALU.mult),
                         reads=[b_pYs[h], bs, b_G], writes=[self.b_xo[i]])
                S.op("dve", lambda e: e.tensor_tensor(out=xo[:], in0=xo[:], in1=xr[:], op=ALU.add),
                     reads=[self.b_xr[i]], writes=[self.b_xo[i]])
                S.dma("sp", f"epo{self.tag}{i}", lambda e: e.dma_start(out=dst, in_=xo[:]), reads=[self.b_xo[i]])

        with ExitStack() as ph:
            cT = sbt(ph, "cT", [128, 16, 2], F32)
            cS = sbt(ph, "cS", [128, 16, 2], BF16)
            bmT = sbt(ph, "bmT", [128, 96], F32)
            gv = sbt(ph, "gv", [128, 4, 16], F32)
            modT = sbt(ph, "modT", [128, 96, 2], F32)
            diag = [sbt(ph, f"diag{i}", [128, 128], F32) for i in range(2)]
            wm = [sbt(ph, f"wm{i}", [128, 16, 512], BF16) for i in range(3)]
            pM = pst(ph, "pM", [128, 96, 2], F32)
            pG = [pst(ph, f"pG{i}", [128, 512], F32) for i in range(2)]
            b_c, b_cS, b_bm, b_gv, b_mod, b_pM = [Buf() for _ in range(6)]
            b_wm = [Buf() for _ in range(3)]
            b_diag = [Buf() for _ in range(2)]
            b_pG = [Buf() for _ in range(2)]
            S.dma("sp", "misc", lambda e: e.dma_start(out=cT[:], in_=cT_in), writes=[b_c])
            S.dma("sp", "misc", lambda e: e.dma_start(out=bmT[:], in_=bmT_in), writes=[b_bm])
            S.dma("sp", "misc", lambda e: e.dma_start(out=gv[:], in_=gv_in), writes=[b_gv])
            S.op("act", lambda e: e.activation(out=cS[:], in_=cT[:], func=AF.Silu), reads=[b_c], writes=[b_cS])
            for n in range(24):
                wt = wm[n % 3]
                S.dma("pool", f"w{n % 3}", lambda e, wt=wt, n=n: e.dma_start(
                    out=wt[:], in_=w_mod[:, n * 512:(n + 1) * 512].rearrange("(k p) n -> p k n", p=128)), writes=[b_wm[n % 3]])

                def mm(e, wt=wt, n=n):
                    r = None
                    for m in range(4):
                        for k in range(16):
                            r = e.matmul(pM[:, n * 4 + m, :], lhsT=wt[:, k, m * 128:(m + 1) * 128], rhs=cS[:, k, :],
                                         start=(k == 0), stop=(k == 15))
                    return r
                S.op("pe", mm, reads=[b_wm[n % 3], b_cS], writes=[b_pM])
            for j in range(2):
                S.op("dve", lambda e, j=j: e.tensor_tensor(out=modT[:, :, j], in0=pM[:, :, j], in1=bmT[:], op=ALU.add),
                     reads=[b_pM, b_bm], writes=[b_mod])
            def mk_scale(dst, seg, col, g):
                S.op("dve", lambda e: e.scalar_tensor_tensor(out=vec[:, dst, :], in0=modT[:, seg * 16:(seg + 1) * 16, col], scalar=1.0,
                                                             in1=gv[:, g, :], op0=ALU.add, op1=ALU.mult), reads=[b_mod, b_gv], writes=[b_vec])

            def mk_copy(dst, seg, col):
                S.op("dve", lambda e: e.tensor_copy(out=vec[:, dst, :], in_=modT[:, seg * 16:(seg + 1) * 16, col]), reads=[b_mod], writes=[b_vec])

            def mk_mul(dst, seg, col, g):
                S.op("dve", lambda e: e.tensor_tensor(out=vec[:, dst, :], in0=modT[:, seg * 16:(seg + 1) * 16, col], in1=gv[:, g, :], op=ALU.mult),
                     reads=[b_mod, b_gv], writes=[b_vec])
            mk_scale(0, 1, 0, 0)
            mk_copy(1, 0, 0)
            mk_scale(2, 1, 1, 0)
            mk_copy(3, 0, 1)
            mk_scale(4, 4, 0, 2)
            mk_copy(5, 3, 0)
            mk_mul(6, 2, 0, 1)
            mk_mul(7, 5, 0, 3)
            cnt = 0
            for gi, Gt in ((6, G1), (7, G2)):
                for k in range(16):
                    dg = diag[cnt % 2]
                    bd = b_diag[cnt % 2]
                    pg = pG[cnt % 2]
                    bp = b_pG[cnt % 2]
                    S.op("dve", lambda e, dg=dg, gi=gi, k=k: e.tensor_scalar(out=dg[:], in0=ident[:], scalar1=vec[:, gi, k:k + 1], scalar2=None,
                                                                             op0=ALU.mult), reads=[b_const, b_vec], writes=[bd])
                    S.op("pe", lambda e, dg=dg, pg=pg: e.matmul(pg[:, 0:128], lhsT=onesf[:], rhs=dg[:], start=True, stop=True),
                         reads=[bd, b_const], writes=[bp])
                    S.op("act", lambda e, Gt=Gt, pg=pg, k=k: e.activation(out=Gt[:, k * 128:(k + 1) * 128], in_=pg[:, 0:128], func=AF.Copy),
                         reads=[bp], writes=[b_G])
                    cnt += 1
            S.barrier()
            S.emit()

        b_WUD = Buf("WUD")

        with ExitStack() as ph:
            sts = split_even(NL, 13)
            MAXT = max(nb for _, nb in sts) * 128
            hT = sbt(ph, "hT", [128, 16, MAXT + CTX], BF16)
            cosT = sbt(ph, "cosT", [128, MAXT], F32)
            sinT = sbt(ph, "sinT", [128, MAXT], F32)
            permf = sbt(ph, "permf", [128, 128], F32)
            NW = 3
            wi = [sbt(ph, f"wi{i}", [128, 16, 512], BF16) for i in range(NW)]
            stg = [sbt(ph, f"stg{i}", [128, MAXT], BF16) for i in range(3)]
            stgc = [sbt(ph, f"stgc{i}", [128, CTX], BF16) for i in range(2)]
            q32 = [sbt(ph, f"q32{i}", [128, 512], F32) for i in range(2)]
            t1 = [sbt(ph, f"t1{i}", [128, 512], F32) for i in range(2)]
            t2 = [sbt(ph, f"t2{i}", [128, 512], F32) for i in range(2)]
            ln = LN(ph, "a")
            pP = [pst(ph, f"pP{i}", [128, 512], F32) for i in range(3)]
            pR = pst(ph, "pR", [128, 512], F32)
            b_hT, b_cs, b_perm, b_pR = Buf(), Buf(), Buf(), Buf()
            b_wi = [Buf() for _ in range(NW)]
            b_stg = [Buf() for _ in range(3)]
            b_stgc = [Buf() for _ in range(2)]
            b_q32 = [Buf() for _ in range(2)]
            b_t1 = [Buf() for _ in range(2)]
            b_t2 = [Buf() for _ in range(2)]
            b_pP = [Buf() for _ in range(3)]
            b_ST = Buf("ST")
            S.dma("sp", "misc", lambda e: e.dma_start(out=permf[:], in_=perm_in), writes=[b_perm])
            ctxmap = {8: 0, 9: 1, 10: 2, 11: 3}
            for i in range(8):
                ctxmap[20 + i] = 4 + i
                ctxmap[28 + i] = 12 + i
            wcnt = 0
            pcnt = 0
            rcnt = 0
            scnt = 0
            sccnt = 0
            for sti, (b0, nb) in enumerate(sts):
                ntok = nb * 128
                tok0 = b0 * 128
                S.dma("sp", "cos", lambda e, tok0=tok0, ntok=ntok: e.dma_start(out=cosT[:, 0:ntok], in_=ropec[:, tok0:tok0 + ntok]), writes=[b_cs])
                S.dma("sp", "sin", lambda e, tok0=tok0, ntok=ntok: e.dma_start(out=sinT[:, 0:ntok], in_=ropes[:, tok0:tok0 + ntok]), writes=[b_cs])
                ln.load(x_loc[tok0:tok0 + 128, :], 128)
                for bi in range(nb):
                    if bi + 1 < nb:
                        pass
                    ln.run(128, vec[:, 0, :], vec[:, 1, :], lambda k, bi=bi: hT[:, k, bi * 128:(bi + 1) * 128], b_hT)
                    if bi + 1 < nb:
                        ln.load(x_loc[tok0 + (bi + 1) * 128:tok0 + (bi + 2) * 128, :], 128)
                with_ctx = (sti == 0)
                if with_ctx:
                    for cb in range(2):
                        ln.load(ctx_in[cb * 128:(cb + 1) * 128, :], 128)
                        ln.run(128, vec[:, 2, :], vec[:, 3, :], lambda k, cb=cb: hT[:, k, MAXT + cb * 128:MAXT + (cb + 1) * 128], b_hT)
                groups = split_even(ntok, 512, 128)
                for wc in range(17):
                    wt = wi[wcnt % NW]
                    bw = b_wi[wcnt % NW]
                    S.dma("pool", f"w{wcnt % NW}", lambda e, wt=wt, wc=wc: e.dma_start(
                        out=wt[:], in_=w_in[:, wc * 512:(wc + 1) * 512].rearrange("(k p) n -> p k n", p=128)), writes=[bw])
                    wcnt += 1
                    for m in range(4):
                        c = wc * 4 + m
                        is_rope = c < 10
                        is_gate = c >= 36
                        sg = stg[scnt % 3]
                        bsg = b_stg[scnt % 3]
                        sgname = f"st{scnt % 3}"
                        scnt += 1
                        pending = []

                        def flush():
                            for (g0, gn, ri) in pending:
                                S.op("pe", lambda e, ri=ri, gn=gn: e.matmul(pR[:, 0:gn], lhsT=permf[:], rhs=q32[ri][:, 0:gn], start=True, stop=True),
                                     reads=[b_q32[ri], b_perm], writes=[b_pR])
                                S.op("dve", lambda e, ri=ri, g0=g0, gn=gn: e.tensor_tensor(out=t1[ri][:, 0:gn], in0=q32[ri][:, 0:gn], in1=cosT[:, g0:g0 + gn], op=ALU.mult),
                                     reads=[b_q32[ri], b_cs], writes=[b_t1[ri]])
                                S.op("dve", lambda e, ri=ri, g0=g0, gn=gn: e.tensor_tensor(out=t2[ri][:, 0:gn], in0=pR[:, 0:gn], in1=sinT[:, g0:g0 + gn], op=ALU.mult),
                                     reads=[b_pR, b_cs], writes=[b_t2[ri]])
                                S.op("dve", lambda e, ri=ri, g0=g0, gn=gn, sg=sg: e.tensor_tensor(out=sg[:, g0:g0 + gn], in0=t1[ri][:, 0:gn], in1=t2[ri][:, 0:gn], op=ALU.add),
                                     reads=[b_t1[ri], b_t2[ri]], writes=[bsg])
                            pending.clear()

                        for (g0, gn) in groups:
                            pp = pP[pcnt % 3]
                            bpp = b_pP[pcnt % 3]
                            pcnt += 1

                            def mm(e, pp=pp, wt=wt, m=m, g0=g0, gn=gn):
                                r = None
                                for k in range(16):
                                    r = e.matmul(pp[:, 0:gn], lhsT=wt[:, k, m * 128:(m + 1) * 128], rhs=hT[:, k, g0:g0 + gn],
                                                 start=(k == 0), stop=(k == 15))
                                return r
                            S.op("pe", mm, reads=[bw, b_hT], writes=[bpp])
                            flush()
                            if is_rope:
                                ri = rcnt % 2
                                rcnt += 1
                                S.op("act", lambda e, pp=pp, ri=ri, gn=gn: e.activation(out=q32[ri][:, 0:gn], in_=pp[:, 0:gn], func=AF.Copy),
                                     reads=[bpp], writes=[b_q32[ri]])
                                pending.append((g0, gn, ri))
                            elif is_gate:
                                S.op("act", lambda e, pp=pp, sg=sg, g0=g0, gn=gn: e.activation(out=sg[:, g0:g0 + gn], in_=pp[:, 0:gn], func=AF.Sigmoid),
                                     reads=[bpp], writes=[bsg])
                            else:
                                S.op("act", lambda e, pp=pp, sg=sg, g0=g0, gn=gn: e.activation(out=sg[:, g0:g0 + gn], in_=pp[:, 0:gn], func=AF.Copy),
                                     reads=[bpp], writes=[bsg])
                        flush()
                        S.dma("sp", sgname, lambda e, sg=sg, c=c, tok0=tok0, ntok=ntok: e.dma_start(out=ST[c, :, tok0:tok0 + ntok], in_=sg[:, 0:ntok]),
                              reads=[bsg], writes=[b_ST])
                        if with_ctx and c in ctxmap:
                            pp = pP[pcnt % 3]
                            bpp = b_pP[pcnt % 3]
                            pcnt += 1
                            sc_ = stgc[sccnt % 2]
                            bsc = b_stgc[sccnt % 2]
                            scname = f"sc{sccnt % 2}"
                            sccnt += 1

                            def mmc(e, pp=pp, wt=wt, m=m):
                                r = None
                                for k in range(16):
                                    r = e.matmul(pp[:, 0:CTX], lhsT=wt[:, k, m * 128:(m + 1) * 128], rhs=hT[:, k, MAXT:MAXT + CTX],
                                                 start=(k == 0), stop=(k == 15))
                                return r
                            S.op("pe", mmc, reads=[bw, b_hT], writes=[bpp])
                            S.op("act", lambda e, pp=pp, sc_=sc_: e.activation(out=sc_[:], in_=pp[:, 0:CTX], func=AF.Copy), reads=[bpp], writes=[bsc])
                            S.dma("sp", scname, lambda e, sc_=sc_, ci=ctxmap[c]: e.dma_start(out=STc[ci, :, :], in_=sc_[:]), reads=[bsc], writes=[b_ST])
            S.barrier()
            S.emit()

        with ExitStack() as ph:
            expG = sbt(ph, "expG", [128, 8, 7, 128], F32)
            Egen = sbt(ph, "Egen", [128, 8, 5, 128], F32)
            Et = [sbt(ph, f"Et{i}", [128, 7, 128], F32) for i in range(2)]
            Mgen = sbt(ph, "Mgen", [128, 5, 128], F32)
            Msp = sbt(ph, "Msp", [128, 4, 7, 128], F32)
            MAp = sbt(ph, "MAp", [128, 3, 1024], F32)
            sinkE = sbt(ph, "sinkE", [128, 8, 128], F32)
            Kr = [sbt(ph, f"Kr{i}", [128, 10, 128], BF16) for i in range(8)]
            Vr = [sbt(ph, f"Vr{i}", [128, 10, 128], BF16) for i in range(8)]
            VTs = [sbt(ph, f"VTs{i}", [128, 10, 128], BF16) for i in range(2)]
            Kc = sbt(ph, "Kc", [128, 10, CTX], BF16)
            Vc = sbt(ph, "Vc", [128, 2, 10, 128], BF16)
            Qt = [sbt(ph, f"Qt{i}", [128, 16, 128], BF16) for i in range(2)]
            S32 = [sbt(ph, f"S32{i}", [128, 1024], F32) for i in range(2)]
            Pb = [sbt(ph, f"Pb{i}", [128, 2560], BF16) for i in range(2)]
            rec = [sbt(ph, f"rec{i}", [128, 512], F32) for i in range(2)]
            OTb = [sbt(ph, f"OTb{i}", [128, 16, 128], BF16) for i in range(2)]
            pV = pst(ph, "pV", [128, 10, 128], BF16)
            pS = [pst(ph, f"pS{i}", [128, 1024], F32) for i in range(2)]
            pOD = [pst(ph, f"pOD{i}", [128, 512], F32) for i in range(2)]
            b_tab, b_Egen, b_sink, b_ctxkv, b_pV = [Buf() for _ in range(5)]
            b_Et = [Buf() for _ in range(2)]
            b_Kr = [Buf() for _ in range(8)]
            b_Vr = [Buf() for _ in range(8)]
            b_VTs = [Buf() for _ in range(2)]
            b_Qt = [Buf() for _ in range(2)]
            b_S32 = [Buf() for _ in range(2)]
            b_Pb = [Buf() for _ in range(2)]
            b_rec = [Buf() for _ in range(2)]
            b_OTb = [Buf() for _ in range(2)]
            b_pS = [Buf() for _ in range(2)]
            b_pOD = [Buf() for _ in range(2)]
            b_OTs = Buf()
            S.dma("sp", "misc", lambda e: e.dma_start(out=expG[:], in_=Gtab_in), writes=[b_tab])
            S.dma("sp", "misc", lambda e: e.dma_start(out=Mgen[:], in_=Mgen_in), writes=[b_tab])
            S.dma("sp", "misc", lambda e: e.dma_start(out=Msp[:], in_=Msp_in), writes=[b_tab])
            S.dma("sp", "misc", lambda e: e.dma_start(out=MAp[:], in_=MA_in), writes=[b_tab])
            S.dma("sp", "misc", lambda e: e.dma_start(out=sinkE[:], in_=sinkb_in), writes=[b_sink])
            S.op("act", lambda e: e.activation(out=expG[:].rearrange("p a b c -> p (a b c)"), in_=expG[:].rearrange("p a b c -> p (a b c)"), func=AF.Exp),
                 reads=[b_tab], writes=[b_tab])
            S.op("act", lambda e: e.activation(out=sinkE[:].rearrange("p a b -> p (a b)"), in_=sinkE[:].rearrange("p a b -> p (a b)"), func=AF.Exp),
                 reads=[b_sink], writes=[b_sink])
            for h in range(8):
                S.op("dve", lambda e, h=h: e.tensor_tensor(out=Egen[:, h, :, :], in0=expG[:, h, 1:6, :], in1=Mgen[:], op=ALU.mult),
                     reads=[b_tab], writes=[b_Egen])
            S.dma("sp", "misc", lambda e: e.dma_start(out=Kc[:, 0:2, :], in_=STc[0:2].rearrange("c p t -> p c t")), reads=[b_ST], writes=[b_ctxkv])
            S.dma("sp", "misc", lambda e: e.dma_start(out=Kc[:, 2:10, :], in_=STc[4:12].rearrange("c p t -> p c t")), reads=[b_ST], writes=[b_ctxkv])
            for cb in range(2):
                vt = VTs[cb]
                S.dma("sp", f"vt{cb}", lambda e, vt=vt, cb=cb: e.dma_start(out=vt[:, 0:2, :], in_=STc[2:4, :, cb * 128:(cb + 1) * 128].rearrange("c p t -> p c t")),
                      reads=[b_ST], writes=[b_VTs[cb]])
                S.dma("sp", f"vt{cb}", lambda e, vt=vt, cb=cb: e.dma_start(out=vt[:, 2:10, :], in_=STc[12:20, :, cb * 128:(cb + 1) * 128].rearrange("c p t -> p c t")),
                      reads=[b_ST], writes=[b_VTs[cb]])

                def trc(e, vt=vt):
                    r = None
                    for k in range(10):
                        r = e.transpose(out=pV[:, k, :], in_=vt[:, k, :], identity=identb[:])
                    return r
                S.op("pe", trc, reads=[b_VTs[cb], b_const], writes=[b_pV])
                S.op("act", lambda e, cb=cb: e.activation(out=Vc[:, cb, :, :], in_=pV[:], func=AF.Copy), reads=[b_pV], writes=[b_ctxkv])

            loaded = [-1]
            vtc = [0]

            def load_kv(b):
                slot = b % 8
                kr, vr = Kr[slot], Vr[slot]
                t0 = b * 128
                S.dma("sp", f"k{slot}", lambda e: e.dma_start(out=kr[:, 0:2, :], in_=ST[8:10, :, t0:t0 + 128].rearrange("c p t -> p c t")),
                      reads=[b_ST], writes=[b_Kr[slot]])
                S.dma("sp", f"k{slot}", lambda e: e.dma_start(out=kr[:, 2:10, :], in_=ST[20:28, :, t0:t0 + 128].rearrange("c p t -> p c t")),
                      reads=[b_ST], writes=[b_Kr[slot]])
                vi = vtc[0] % 2
                vtc[0] += 1
                vt = VTs[vi]
                S.dma("sp", f"vt{vi}", lambda e: e.dma_start(out=vt[:, 0:2, :], in_=ST[10:12, :, t0:t0 + 128].rearrange("c p t -> p c t")),
                      reads=[b_ST], writes=[b_VTs[vi]])
                S.dma("sp", f"vt{vi}", lambda e: e.dma_start(out=vt[:, 2:10, :], in_=ST[28:36, :, t0:t0 + 128].rearrange("c p t -> p c t")),
                      reads=[b_ST], writes=[b_VTs[vi]])

                def trv(e):
                    r = None
                    for k in range(10):
                        r = e.transpose(out=pV[:, k, :], in_=vt[:, k, :], identity=identb[:])
                    return r
                S.op("pe", trv, reads=[b_VTs[vi], b_const], writes=[b_pV])
                S.op("act", lambda e: e.activation(out=vr[:], in_=pV[:], func=AF.Copy), reads=[b_pV], writes=[b_Vr[slot]])

            def load_q(j, qi):
                qt = Qt[qi]
                t0 = j * 128
                S.dma("sp", f"q{qi}", lambda e: e.dma_start(out=qt[:, 0:8, :], in_=ST[0:8, :, t0:t0 + 128].rearrange("c p t -> p c t")),
                      reads=[b_ST], writes=[b_Qt[qi]])
                S.dma("sp", f"q{qi}", lambda e: e.dma_start(out=qt[:, 8:16, :], in_=ST[12:20, :, t0:t0 + 128].rearrange("c p t -> p c t")),
                      reads=[b_ST], writes=[b_Qt[qi]])

            ucnt = [0]
            pcn = [0]

            pend = []

            def stage2(ctx_):
                (u, NQ, chunks, P, bP, sink_ap, out_ap, b_out, after) = ctx_
                nch = len(chunks)
                od = pOD[u]
                bod = b_pOD[u]
                vbs = [bb for ci in range(nch) for bb in chunks[ci][3]]
                if NQ == 128:
                    o_ps, d_ps, bo, bd = od[:, 0:128], od[:, 128:256], bod, bod
                else:
                    o_ps, d_ps, bo, bd = od[:, 0:512], pOD[1 - u][:, 0:512], bod, b_pOD[1 - u]

                def pv(e):
                    r = None
                    for ci in range(nch):
                        r = e.matmul(o_ps, lhsT=chunks[ci][1], rhs=P[:, ci * NQ:(ci + 1) * NQ], start=(ci == 0), stop=(ci == nch - 1))
                    for ci in range(nch):
                        r = e.matmul(d_ps, lhsT=onesb[:], rhs=P[:, ci * NQ:(ci + 1) * NQ], start=(ci == 0), stop=(ci == nch - 1))
                    return r
                S.op("pe", pv, reads=[bP, b_const] + vbs, writes=[bo] if bo is bd else [bo, bd])
                rc, brc = rec[u], b_rec[u]
                if sink_ap is not None:
                    S.op("dve", lambda e: e.tensor_tensor(out=rc[:, 0:NQ], in0=d_ps, in1=sink_ap, op=ALU.add), reads=[bd, b_sink], writes=[brc])
                    S.op("dve", lambda e: e.reciprocal(out=rc[:, 0:NQ], in_=rc[:, 0:NQ]), reads=[brc], writes=[brc])
                else:
                    S.op("dve", lambda e: e.reciprocal(out=rc[:, 0:NQ], in_=d_ps), reads=[bd], writes=[brc])
                S.op("dve", lambda e: e.tensor_tensor(out=out_ap, in0=o_ps, in1=rc[:, 0:NQ], op=ALU.mult), reads=[bo, bd, brc], writes=[b_out])
                if after is not None:
                    after()

            def attn_unit(q_ap, NQ, chunks, nmask, mask_ap, mask_bufs, sink_ap, out_ap, b_q, b_out, after=None):
                u = ucnt[0] % 2
                ucnt[0] += 1
                P = Pb[u]
                bP = b_Pb[u]
                cap = 1024 // NQ
                nch = len(chunks)
                pieces = [list(range(i, min(i + cap, nch))) for i in range(0, nch, cap)]
                for piece in pieces:
                    si = pcn[0] % 2
                    pcn[0] += 1
                    ps, bps = pS[si], b_pS[si]
                    s32, bs32 = S32[si], b_S32[si]

                    def smm(e, piece=piece, ps=ps):
                        r = None
                        for li, ci in enumerate(piece):
                            r = e.matmul(ps[:, li * NQ:(li + 1) * NQ], lhsT=chunks[ci][0], rhs=q_ap, start=True, stop=True)
                        return r
                    S.op("pe", smm, reads=[b_q] + [bb for ci in piece for bb in chunks[ci][2]], writes=[bps])
                    nm = len([ci for ci in piece if ci < nmask])
                    c0 = piece[0]
                    n2 = len(piece)
                    if nm > 0:
                        S.op("act", lambda e, ps=ps, s32=s32, nm=nm: e.activation(out=s32[:, 0:nm * NQ], in_=ps[:, 0:nm * NQ], func=AF.Exp, scale=SCALE),
                             reads=[bps], writes=[bs32])
                        S.op("dve", lambda e, s32=s32, nm=nm, c0=c0: e.tensor_tensor(out=P[:, c0 * NQ:(c0 + nm) * NQ], in0=s32[:, 0:nm * NQ],
                                                                                     in1=mask_ap[:, c0 * NQ:(c0 + nm) * NQ], op=ALU.mult),
                             reads=[bs32] + mask_bufs, writes=[bP])
                    if nm < n2:
                        S.op("act", lambda e, ps=ps, nm=nm, n2=n2, c0=c0: e.activation(out=P[:, (c0 + nm) * NQ:(c0 + n2) * NQ], in_=ps[:, nm * NQ:n2 * NQ],
                                                                                       func=AF.Exp, scale=SCALE), reads=[bps], writes=[bP])
                pend.append((u, NQ, chunks, P, bP, sink_ap, out_ap, b_out, after))
                while len(pend) > 1:
                    stage2(pend.pop(0))

            special = {3: 0, 4: 1, NL - 5: 2, NL - 4: 3}
            mixer_blocks = list(range(2, NL - 2))
            load_q(mixer_blocks[0], 0)
            ecnt = 0
            for idx, j in enumerate(mixer_blocks):
                qi = idx % 2
                while loaded[0] < min(j + 3, NL - 1):
                    loaded[0] += 1
                    load_kv(loaded[0])
                if idx + 1 < len(mixer_blocks):
                    load_q(mixer_blocks[idx + 1], 1 - qi)
                qt = Qt[qi]
                ot = OTb[qi]
                bot = b_OTb[qi]
                mpair = 1 if j == 3 else (2 if j == NL - 4 else 0)
                for g in range(2):
                    chunks = []
                    for bb in (j - 1, j + 1, j):
                        chunks.append((Kr[bb % 8][:, g, :], Vr[bb % 8][:, g, :], [b_Kr[bb % 8]], [b_Vr[bb % 8]]))
                    for cb in range(2):
                        chunks.append((Kc[:, g, cb * 128:(cb + 1) * 128], Vc[:, cb, g, :], [b_ctxkv], [b_ctxkv]))
                    attn_unit(qt[:, g * 4:(g + 1) * 4, :].rearrange("p a b -> p (a b)"), 512, chunks, 2, MAp[:, mpair, :], [b_tab],
                              sinkE[:, g * 4:(g + 1) * 4, :].rearrange("p a b -> p (a b)"),
                              ot[:, g * 4:(g + 1) * 4, :].rearrange("p a b -> p (a b)"), b_Qt[qi], bot)
                offs = list(range(-3, 4)) if j in special else list(range(-2, 3))
                for h in range(8):
                    if j in special:
                        et, bet = Et[ecnt % 2], b_Et[ecnt % 2]
                        ecnt += 1
                        S.op("dve", lambda e, h=h, et=et, si=special[j]: e.tensor_tensor(out=et[:], in0=expG[:, h, :, :], in1=Msp[:, si, :, :], op=ALU.mult),
                             reads=[b_tab], writes=[bet])
                        mask_ap, mbufs = et[:].rearrange("p a b -> p (a b)"), [bet]
                    else:
                        mask_ap, mbufs = Egen[:, h, :, :].rearrange("p a b -> p (a b)"), [b_Egen]
                    chunks = []
                    for o in offs:
                        bb = j + o
                        chunks.append((Kr[bb % 8][:, 2 + h, :], Vr[bb % 8][:, 2 + h, :], [b_Kr[bb % 8]], [b_Vr[bb % 8]]))
                    for cb in range(2):
                        chunks.append((Kc[:, 2 + h, cb * 128:(cb + 1) * 128], Vc[:, cb, 2 + h, :], [b_ctxkv], [b_ctxkv]))
                    after = None
                    if h == 7:
                        m0 = (j - 2) * 128

                        def after(ot=ot, m0=m0, qi=qi, bot=bot):
                            S.dma("sp", f"ot{qi}", lambda e: e.dma_start(out=OTs[:, :, m0:m0 + 128].rearrange("c p t -> p c t"), in_=ot[:]),
                                  reads=[bot], writes=[b_OTs])
                    attn_unit(qt[:, 8 + h, :], 128, chunks, len(offs), mask_ap, mbufs, None, ot[:, 8 + h, :], b_Qt[qi], bot, after)
            while pend:
                stage2(pend.pop(0))
            S.barrier()
            S.emit()

        b_ZT = Buf()
        b_X1 = Buf()
        mgroups = [(o * 128, n * 128) for (o, n) in split_even(NM, 4)]
        with ExitStack() as ph:
            wbr = sbt(ph, "wbr", [128, 16, D], BF16)
            OTl = [sbt(ph, f"OTl{i}", [128, 16, 512], BF16) for i in range(2)]
            gl = [sbt(ph, f"gl{i}", [128, 2, 512], BF16) for i in range(4)]
            ta = [sbt(ph, f"ta{i}", [128, 512], F32) for i in range(2)]
            tb = [sbt(ph, f"tb{i}", [128, 512], F32) for i in range(2)]
            zst = [sbt(ph, f"zst{i}", [128, 16, 512], BF16) for i in range(2)]
            pA = [pst(ph, f"pA{i}", [128, 512], F32) for i in range(2)]
            pB = [pst(ph, f"pB{i}", [128, 512], F32) for i in range(2)]
            b_wbr = Buf()
            b_OTl = [Buf() for _ in range(2)]
            b_gl = [[Buf(), Buf()] for _ in range(4)]
            b_ta = [Buf() for _ in range(2)]
            b_tb = [Buf() for _ in range(2)]
            b_zst = [Buf() for _ in range(2)]
            b_pA = [Buf() for _ in range(2)]
            b_pB = [Buf() for _ in range(2)]
            for i in range(4):
                S.dma("pool", f"w{i % 3}", lambda e, i=i: e.dma_start(out=wbr[:, 0:8, i * 512:(i + 1) * 512],
                                                                      in_=w_br_a[:, i * 512:(i + 1) * 512].rearrange("(k p) n -> p k n", p=128)), writes=[b_wbr])
                S.dma("pool", f"w{(i + 1) % 3}", lambda e, i=i: e.dma_start(out=wbr[:, 8:16, i * 512:(i + 1) * 512],
                                                                            in_=w_br_b[:, i * 512:(i + 1) * 512].rearrange("(k p) n -> p k n", p=128)), writes=[b_wbr])
            gcnt = 0
            fc = 0
            for gi, (m0, gn) in enumerate(mgroups):
                oi = gi % 2
                otl = OTl[oi]
                S.dma("sp", f"otl{oi}", lambda e, otl=otl, m0=m0, gn=gn: e.dma_start(out=otl[:, :, 0:gn], in_=OTs[:, :, m0:m0 + gn].rearrange("c p t -> p c t")),
                      reads=[b_OTs], writes=[b_OTl[oi]])
                zs = zst[oi]
                for f in range(16):
                    g_ = gl[gcnt % 4]
                    bg = b_gl[gcnt % 4]
                    gname = f"gl{gcnt % 4}"
                    gcnt += 1
                    l0 = m0 + 256
                    S.dma("sp", gname + "a", lambda e, g_=g_, f=f, l0=l0, gn=gn: e.dma_start(out=g_[:, 0, 0:gn], in_=ST[36 + f, :, l0:l0 + gn]), reads=[b_ST], writes=[bg[0]])
                    S.dma("sp", gname + "b", lambda e, g_=g_, f=f, l0=l0, gn=gn: e.dma_start(out=g_[:, 1, 0:gn], in_=ST[52 + f, :, l0:l0 + gn]), reads=[b_ST], writes=[bg[1]])
                    pa, pb_, bpa, bpb = pA[fc % 2], pB[fc % 2], b_pA[fc % 2], b_pB[fc % 2]
                    ta_, tb_, bta, btb = ta[fc % 2], tb[fc % 2], b_ta[fc % 2], b_tb[fc % 2]
                    fc += 1

                    def mmA(e, pa=pa, f=f, otl=otl, gn=gn):
                        r = None
                        for k in range(8):
                            r = e.matmul(pa[:, 0:gn], lhsT=wbr[:, k, f * 128:(f + 1) * 128], rhs=otl[:, k, 0:gn], start=(k == 0), stop=(k == 7))
                        return r

                    def mmB(e, pb_=pb_, f=f, otl=otl, gn=gn):
                        r = None
                        for k in range(8):
                            r = e.matmul(pb_[:, 0:gn], lhsT=wbr[:, 8 + k, f * 128:(f + 1) * 128], rhs=otl[:, 8 + k, 0:gn], start=(k == 0), stop=(k == 7))
                        return r
                    S.op("pe", mmA, reads=[b_wbr, b_OTl[oi]], writes=[bpa])
                    S.op("pe", mmB, reads=[b_wbr, b_OTl[oi]], writes=[bpb])
                    S.op("dve", lambda e, ta_=ta_, pa=pa, g_=g_, gn=gn: e.tensor_tensor(out=ta_[:, 0:gn], in0=pa[:, 0:gn], in1=g_[:, 0, 0:gn], op=ALU.mult),
                         reads=[bpa, bg[0]], writes=[bta])
                    S.op("dve", lambda e, tb_=tb_, pb_=pb_, g_=g_, gn=gn: e.tensor_tensor(out=tb_[:, 0:gn], in0=pb_[:, 0:gn], in1=g_[:, 1, 0:gn], op=ALU.mult),
                         reads=[bpb, bg[1]], writes=[btb])
                    S.op("dve", lambda e, zs=zs, f=f, ta_=ta_, tb_=tb_, gn=gn: e.tensor_tensor(out=zs[:, f, 0:gn], in0=ta_[:, 0:gn], in1=tb_[:, 0:gn], op=ALU.add),
                         reads=[bta, btb], writes=[b_zst[oi]])
                S.dma("sp", f"zs{oi}", lambda e, zs=zs, m0=m0, gn=gn: e.dma_start(out=ZT[:, :, m0:m0 + gn].rearrange("c p t -> p c t"), in_=zs[:, :, 0:gn]),
                      reads=[b_zst[oi]], writes=[b_ZT])
            S.barrier()
            S.emit()

        with ExitStack() as ph:
            wo = sbt(ph, "wo", [128, 16, D], BF16)
            zl = [sbt(ph, f"zl{i}", [128, 16, 512], BF16) for i in range(2)]
            epi = Epi(ph, "m")
            pY = [pst(ph, f"pY{i}", [128, 1024], F32) for i in range(4)]
            b_wo = Buf()
            b_zl = [Buf() for _ in range(2)]
            b_pY = [Buf() for _ in range(4)]
            for i in range(4):
                S.dma("pool", f"w{i % 3}", lambda e, i=i: e.dma_start(out=wo[:, :, i * 512:(i + 1) * 512],
                                                                      in_=w_o[:, i * 512:(i + 1) * 512].rearrange("(k p) n -> p k n", p=128)), writes=[b_wo])
            bc = 0
            for gi, (m0, gn) in enumerate(mgroups):
                zi = gi % 2
                z_ = zl[zi]
                S.dma("sp", f"zl{zi}", lambda e, z_=z_, m0=m0, gn=gn: e.dma_start(out=z_[:, :, 0:gn], in_=ZT[:, :, m0:m0 + gn].rearrange("c p t -> p c t")),
                      reads=[b_ZT], writes=[b_zl[zi]])
                for bi in range(gn // 128):
                    mb = m0 // 128 + bi
                    lj = mb + 2
                    pys = [pY[(bc % 2) * 2], pY[(bc % 2) * 2 + 1]]
                    bpys = [b_pY[(bc % 2) * 2], b_pY[(bc % 2) * 2 + 1]]
                    bc += 1
                    epi.load_res(x_loc[lj * 128:(lj + 1) * 128, :])
                    for hh in range(2):
                        def mmo(e, hh=hh, z_=z_, bi=bi, py=pys[hh]):
                            r = None
                            for n in range(2):
                                for k in range(16):
                                    r = e.matmul(py[:, n * 512:(n + 1) * 512], lhsT=z_[:, k, bi * 128:(bi + 1) * 128],
                                                 rhs=wo[:, k, (hh * 2 + n) * 512:(hh * 2 + n + 1) * 512], start=(k == 0), stop=(k == 15))
                            return r
                        S.op("pe", mmo, reads=[b_wo, b_zl[zi]], writes=[bpys[hh]])
                    epi.run([pys[0][:], pys[1][:]], bpys, G1, X1[mb * 128:(mb + 1) * 128, :])
            S.barrier()
            S.emit()

        with ExitStack() as ph:
            T = 512
            h2T = sbt(ph, "h2T", [128, 16, T + 2], BF16)
            actT = sbt(ph, "actT", [128, NF, T], BF16)
            y2T = sbt(ph, "y2T", [128, 16, T], F32)
            cw = sbt(ph, "cw", [128, 88, 3], F32)
            cb_ = sbt(ph, "cb", [128, 88], F32)
            edge = sbt(ph, "edge", [128, 2], F32)
            wu = [sbt(ph, f"wu{i}", [128, 16, 256], BF16) for i in range(3)]
            wd = [sbt(ph, f"wd{i}", [128, NF, 128], BF16) for i in range(2)]
            tA = [sbt(ph, f"tA{i}", [128, 512], F32) for i in range(2)]
            tG = [sbt(ph, f"tG{i}", [128, 512], F32) for i in range(2)]
            ln = LN(ph, "f", nbuf=1)
            epi = Epi(ph, "f", nbuf=1)
            pU = [pst(ph, f"pU{i}", [128, 1024], F32) for i in range(2)]
            pZ = [pst(ph, f"pZ{i}", [128, 512], F32) for i in range(2)]
            b_h2T, b_actT, b_y2T, b_cw = Buf(), Buf(), Buf(), Buf()
            b_wu = [Buf() for _ in range(3)]
            b_wd = [Buf() for _ in range(2)]
            b_tA = [Buf() for _ in range(2)]
            b_tG = [Buf() for _ in range(2)]
            b_pU = [Buf() for _ in range(2)]
            b_pZ = [Buf() for _ in range(2)]
            S.dma("sp", "misc", lambda e: e.dma_start(out=cw[:], in_=cwT_in), writes=[b_cw])
            S.dma("sp", "misc", lambda e: e.dma_start(out=cb_[:], in_=cbT_in), writes=[b_cw])
            S.dma("sp", "misc", lambda e: e.dma_start(out=edge[:], in_=edge_in), writes=[b_cw])
            ntile = NOT // T
            wuc = 0
            wdc = 0
            tc_ = 0
            for ti in range(ntile):
                s = 128 + ti * T
                r0 = s - 1
                nrow_tiles = [(i * 128, 128) for i in range(T // 128)] + [(T, 2)]
                ln.load(X1[r0:r0 + 128, :], 128)
                for ri, (ro, rn) in enumerate(nrow_tiles):
                    ln.run(rn, vec[:, 4, :], vec[:, 5, :], lambda k, ro=ro, rn=rn: h2T[:, k, ro:ro + rn], b_h2T)
                    if ri + 1 < len(nrow_tiles):
                        ro2, rn2 = nrow_tiles[ri + 1]
                        ln.load(X1[r0 + ro2:r0 + ro2 + rn2, :], rn2)
                if ti == 0:
                    S.op("dve", lambda e: e.tensor_scalar(out=h2T[:, :, 0:1], in0=h2T[:, :, 0:1], scalar1=edge[:, 0:1], scalar2=None, op0=ALU.mult),
                         reads=[b_cw], writes=[b_h2T])
                if ti == ntile - 1:
                    S.op("dve", lambda e: e.tensor_scalar(out=h2T[:, :, T + 1:T + 2], in0=h2T[:, :, T + 1:T + 2], scalar1=edge[:, 1:2], scalar2=None, op0=ALU.mult),
                         reads=[b_cw], writes=[b_h2T])
                for f in range(NF):
                    w_ = wu[wuc % 3]
                    bw = b_wu[wuc % 3]
                    wname = f"w{wuc % 3}"
                    wuc += 1
                    if ti == 0:
                        S.dma("pool", wname, lambda e, w_=w_, f=f: e.dma_start(out=w_[:, :, 0:128], in_=w_up[:, f * 128:(f + 1) * 128].rearrange("(k p) n -> p k n", p=128)),
                              writes=[bw])
                        S.dma("pool", wname, lambda e, w_=w_, f=f: e.dma_start(out=w_[:, :, 128:256],
                                                                               in_=w_up[:, DFF + f * 128:DFF + (f + 1) * 128].rearrange("(k p) n -> p k n", p=128)),
                              writes=[bw])
                        S.dma("sp", "s" + wname, lambda e, w_=w_, f=f: e.dma_start(out=WU[f], in_=w_[:]), reads=[bw], writes=[b_WUD])
                    else:
                        S.dma("pool", wname, lambda e, w_=w_, f=f: e.dma_start(out=w_[:], in_=WU[f]), reads=[b_WUD], writes=[bw])
                    for part in range(2):
                        pu, bpu = pU[part], b_pU[part]

                        def mmu(e, pu=pu, w_=w_, part=part):
                            r = None
                            for k in range(16):
                                r = e.matmul(pu[:, 254:512], lhsT=w_[:, k, part * 128:(part + 1) * 128], rhs=h2T[:, k, 0:258], start=(k == 0), stop=(k == 15))
                            for k in range(16):
                                r = e.matmul(pu[:, 512:770], lhsT=w_[:, k, part * 128:(part + 1) * 128], rhs=h2T[:, k, 256:514], start=(k == 0), stop=(k == 15))
                            return r
                        S.op("pe", mmu, reads=[bw, b_h2T], writes=[bpu])
                        fc_ = part * NF + f
                        tt = (tA if part == 0 else tG)[tc_ % 2]
                        btt = (b_tA if part == 0 else b_tG)[tc_ % 2]
                        U = [pu[:, 254 + sft:254 + sft + 516].rearrange("p (h i) -> p h i", h=2)[:, :, 0:256] for sft in range(3)]
                        tv = tt[:].rearrange("p (h i) -> p h i", h=2)
                        S.op("act", lambda e, U=U, tv=tv, fc_=fc_: e.activation(out=tv, in_=U[0], func=AF.Identity, scale=cw[:, fc_, 0:1], bias=cb_[:, fc_:fc_ + 1]),
                             reads=[bpu, b_cw], writes=[btt])
                        S.op("dve", lambda e, U=U, tv=tv, fc_=fc_: e.scalar_tensor_tensor(out=tv, in0=U[1], scalar=cw[:, fc_, 1:2], in1=tv, op0=ALU.mult, op1=ALU.add),
                             reads=[bpu, b_cw], writes=[btt])
                        S.op("dve", lambda e, U=U, tv=tv, fc_=fc_: e.scalar_tensor_tensor(out=tv, in0=U[2], scalar=cw[:, fc_, 2:3], in1=tv, op0=ALU.mult, op1=ALU.add),
                             reads=[bpu, b_cw], writes=[btt])
                    ta_, tg_ = tA[tc_ % 2], tG[tc_ % 2]
                    bta, btg = b_tA[tc_ % 2], b_tG[tc_ % 2]
                    tc_ += 1
                    S.op("act", lambda e, tg_=tg_: e.activation(out=tg_[:], in_=tg_[:], func=AF.Silu), reads=[btg], writes=[btg])
                    S.op("dve", lambda e, ta_=ta_, tg_=tg_, f=f: e.tensor_tensor(out=actT[:, f, :], in0=ta_[:], in1=tg_[:], op=ALU.mult),
                         reads=[bta, btg], writes=[b_actT])
                for o in range(16):
                    w_ = wd[wdc % 2]
                    bw = b_wd[wdc % 2]
                    wname = f"wd{wdc % 2}"
                    wdc += 1
                    if ti == 0:
                        S.dma("pool", wname, lambda e, w_=w_, o=o: e.dma_start(out=w_[:], in_=w_down[:, o * 128:(o + 1) * 128].rearrange("(k p) n -> p k n", p=128)),
                              writes=[bw])
                        S.dma("sp", "s" + wname, lambda e, w_=w_, o=o: e.dma_start(out=WD[o], in_=w_[:]), reads=[bw], writes=[b_WUD])
                    else:
                        S.dma("pool", wname, lambda e, w_=w_, o=o: e.dma_start(out=w_[:], in_=WD[o]), reads=[b_WUD], writes=[bw])
                    pz, bpz = pZ[o % 2][:], b_pZ[o % 2]

                    def mmd(e, pz=pz, w_=w_):
                        r = None
                        for k in range(NF):
                            r = e.matmul(pz, lhsT=w_[:, k, :], rhs=actT[:, k, :], start=(k == 0), stop=(k == NF - 1))
                        return r
                    S.op("pe", mmd, reads=[bw, b_actT], writes=[bpz])
                    S.op("act", lambda e, pz=pz, o=o: e.activation(out=y2T[:, o, :], in_=pz, func=AF.Copy), reads=[bpz], writes=[b_y2T])
                for bi in range(T // 128):
                    epi.load_res(X1[s + bi * 128:s + (bi + 1) * 128, :])
                    halves = [pU[0], pU[1]]
                    bh = [b_pU[0], b_pU[1]]
                    for hh in range(2):
                        def trb(e, hh=hh, bi=bi, ph_=halves[hh]):
                            r = None
                            for o in range(8):
                                r = e.matmul(ph_[:, o * 128:(o + 1) * 128], lhsT=y2T[:, hh * 8 + o, bi * 128:(bi + 1) * 128], rhs=ident[:], start=True, stop=True)
                            return r
                        S.op("pe", trb, reads=[b_y2T, b_const], writes=[bh[hh]])
                    oo = ti * T + bi * 128
                    epi.run([halves[0][:], halves[1][:]], bh, G2, out[oo:oo + 128, :])
            S.barrier()
            S.emit()
    return nc


def _fm(v, k=16):
    return np.ascontiguousarray(np.asarray(v, np.float32).reshape(k, 128).T)


def _geometry_tables(NOWN, half):
    NL = NOWN + 6
    NBS = 2 * NOWN
    rows = NBS * 2
    SEQ = NBS * 128
    kk = np.arange(128)
    kr, ck = kk // 64, kk % 64
    qr, cq = kk // 64, kk % 64
    col_start = np.clip(cq - NB_COLS // 2, 0, GRID_W - NB_COLS)
    colvalid = (ck[:, None] >= col_start[None, :]) & (ck[:, None] < col_start[None, :] + NB_COLS)

    def valid(g, o):
        gb = g + o
        if gb < 0 or gb >= NBS or g < 0 or g >= NBS:
            return np.zeros((128, 128), np.float32)
        krow = 2 * gb + kr
        r = 2 * g + qr
        start = np.clip(r - NB_ROWS // 2, 0, rows - NB_ROWS)
        rowvalid = (krow[:, None] >= start[None, :]) & (krow[:, None] < start[None, :] + NB_ROWS)
        return (rowvalid & colvalid).astype(np.float32)

    g0 = NBS // 2
    Mgen = np.stack([valid(g0, o) for o in range(-2, 3)], axis=1)
    spj = [3, 4, NL - 5, NL - 4]
    Msp = np.stack([np.stack([valid(half * NOWN - 3 + j, o) for o in range(-3, 4)], axis=1) for j in spj], axis=1)
    triP = (kk[:, None] >= kk[None, :]).astype(np.float32)
    triN = (kk[:, None] <= kk[None, :]).astype(np.float32)
    z = np.zeros_like(triP)
    g3 = half * NOWN - 3 + 3
    gl = half * NOWN - 3 + (NL - 4)
    pairs = [(triP, triN), (triP if g3 - 1 >= 0 else z, triN), (triP, triN if gl + 1 < NBS else z)]
    MA = np.stack([np.concatenate([np.tile(a, (1, 4)), np.tile(b, (1, 4))], axis=1) for a, b in pairs], axis=1)
    dr = np.stack([np.clip(2 * o + kr[:, None] - qr[None, :] + (NB_ROWS - 1), 0, 2 * NB_ROWS - 2) for o in range(-3, 4)], axis=0)
    dc = np.clip(ck[:, None] - cq[None, :], -(NB_COLS - 1), NB_COLS - 1) + (NB_COLS - 1)
    t = (half * NOWN - 3) * 128 + np.arange(NL * 128)
    t = np.clip(t, 0, SEQ - 1)
    row, col = (t // GRID_W).astype(np.float32), (t % GRID_W).astype(np.float32)
    inv = (ROPE_BASE ** (-np.arange(32, dtype=np.float32) / 32)).astype(np.float32)
    p = np.arange(128)
    pos = np.where((p < 64)[:, None], row[None, :], col[None, :]).astype(np.float32)
    ang = (pos * inv[p % 32][:, None]).astype(np.float32)
    cosT = np.cos(ang).astype(np.float32)
    sgn = np.where((p % 64) < 32, -1.0, 1.0).astype(np.float32)[:, None]
    sinT = (np.sin(ang).astype(np.float32) * sgn).astype(np.float32)
    partner = np.where((p % 64) < 32, p + 32, p - 32)
    perm = np.zeros((128, 128), np.float32)
    perm[partner, p] = 1.0
    edge = np.tile(np.array([[1.0 if half == 1 else 0.0, 1.0 if half == 0 else 0.0]], np.float32), (128, 1))
    return dict(Mgen=Mgen, Msp=Msp, MA=MA, dr=dr, dc=dc, ropec=cosT, ropes=sinT, perm=perm, edge=edge)


_NC_CACHE = {}


def kernel(x, c, ctx, c_ctx, w_mod, b_mod, g_attn_pre, g_attn_post, g_ffn_pre, g_ffn_post,
           w_in, sink_a, rpb_b, w_br_a, w_br_b, w_o, w_up, conv_w, conv_b, w_down):
    x = np.asarray(x, np.float32)
    B, SEQ, _ = x.shape
    NBS = SEQ // 128
    NOWN = NBS // 2
    NL = NOWN + 6
    f32 = lambda a: np.ascontiguousarray(np.asarray(a, np.float32))
    w_mod0, w_in0, w_br_a0, w_br_b0, w_o0, w_up0, w_down0 = [f32(w[0]) for w in (w_mod, w_in, w_br_a, w_br_b, w_o, w_up, w_down)]
    bmT = _fm(np.asarray(b_mod)[0], 96)
    gv = np.ascontiguousarray(np.stack([_fm(np.asarray(g)[0]) for g in (g_attn_pre, g_attn_post, g_ffn_pre, g_ffn_post)], axis=1))
    cw = np.asarray(conv_w, np.float32)[0]
    cwT = np.ascontiguousarray(cw.reshape(3, 88, 128).transpose(2, 1, 0))
    cbT = _fm(np.asarray(conv_b)[0], 88)
    sinkb = np.ascontiguousarray(np.broadcast_to(np.asarray(sink_a, np.float32)[0][None, :, None], (128, 8, 128)))
    rpb = np.asarray(rpb_b, np.float32)[0]
    ident = np.eye(128, dtype=np.float32)
    geo = [_geometry_tables(NOWN, h) for h in range(2)]
    Gt = rpb[:, geo[0]["dr"], geo[0]["dc"][None]]
    Gtab = np.ascontiguousarray(Gt.transpose(2, 0, 1, 3))
    in_maps = []
    for core in range(8):
        b, half = core // 2, core % 2
        g = geo[half]
        xl = np.zeros((NL * 128, D), np.float32)
        lo = (half * NOWN - 3) * 128
        s0, s1 = max(lo, 0), min(lo + NL * 128, SEQ)
        xl[s0 - lo:s1 - lo] = x[b, s0:s1]
        cT = np.ascontiguousarray(np.stack([_fm(np.asarray(c)[b]), _fm(np.asarray(c_ctx))], axis=2))
        in_maps.append(dict(
            x_loc=xl, ctx=f32(np.asarray(ctx)[b]), cT=cT, bmT=bmT, gv=gv, w_mod=w_mod0, w_in=w_in0, w_br_a=w_br_a0, w_br_b=w_br_b0,
            w_o=w_o0, w_up=w_up0, w_down=w_down0, cwT=cwT, cbT=cbT, sinkb=sinkb, Gtab=Gtab, Mgen=f32(g["Mgen"]), Msp=f32(g["Msp"]),
            MA=f32(g["MA"]), ropec=f32(g["ropec"]), ropes=f32(g["ropes"]), perm=g["perm"], ident=ident, edge=f32(g["edge"])))
    if NOWN not in _NC_CACHE:
        _NC_CACHE[NOWN] = build(NOWN)
    nc = _NC_CACHE[NOWN]
    res = run_bass_kernel_spmd(nc, in_maps, core_ids=list(range(8)))
    out = np.zeros((B, SEQ, D), np.float32)
    for core in range(8):
        b, half = core // 2, core % 2
        out[b, half * NOWN * 128:(half + 1) * NOWN * 128] = res.results[core]["out"]
    return out
```
